# Optimizing a Trainium2 kernel written in Bass

```python
import math
import jax, jax.numpy as jnp
from jax import lax
import numpy as np

D_MODEL = 1024
BATCH = 16
SEQ = 2048
DEPTH = 4

GRID_W = 64
N_EVEN = (DEPTH + 1) // 2
N_ODD = DEPTH // 2
EPS = 1e-6

DA_HEADS = 4
DA_HEAD_DIM = 64
DA_WIDTH = DA_HEADS * 2 * DA_HEAD_DIM
Q_BLOCK = 128
T5_BUCKETS = 32
T5_MAX_DIST = 128
LRU_WIDTH = D_MODEL // 2
LRU_BLOCKS = 8
LRU_BLOCK = LRU_WIDTH // LRU_BLOCKS
CONV_W = 4
CONV_PAD = (2, 1)
LRU_C = 8.0
EVEN_IN = 4 * DA_WIDTH + 2 * LRU_WIDTH
EVEN_MIX = DA_WIDTH + LRU_WIDTH
NA_HEADS = 16
NA_HEAD_DIM = 64
NA_WIDTH = NA_HEADS * NA_HEAD_DIM
NA_WIN_R = 8
NA_WIN_C = 16
ODD_IN = 4 * NA_WIDTH

kernel_name = "hybrid_diffattn_rglru_natten_encoder"


def rms_norm(x, g):
    xf = x.astype(jnp.float32)
    y = xf * lax.rsqrt(jnp.mean(xf * xf, axis=-1, keepdims=True) + EPS)
    return (y * g.astype(jnp.float32)).astype(x.dtype)


def t5_bucket(rel):
    nb = T5_BUCKETS // 2
    max_exact = nb // 2
    ret = jnp.where(rel > 0, nb, 0)
    n = jnp.abs(rel)
    nf = jnp.maximum(n, 1).astype(jnp.float32)
    large = max_exact + (jnp.log(nf / max_exact) / math.log(T5_MAX_DIST / max_exact)
                         * (nb - max_exact)).astype(jnp.int32)
    large = jnp.minimum(large, nb - 1)
    return ret + jnp.where(n < max_exact, n, large)


def diff_attention(q, k, v, t5_table, lam, lam_init, subln_g):
    B, S = q.shape[0], q.shape[1]
    H, d = DA_HEADS, DA_HEAD_DIM
    scale = d ** -0.5
    n_blk = S // Q_BLOCK
    qh = q.transpose(0, 2, 3, 1, 4)
    kh = k.transpose(0, 2, 3, 1, 4)
    vh = v.transpose(0, 2, 1, 3)
    qb = qh.reshape(B, H, 2, n_blk, Q_BLOCK, d).transpose(3, 0, 1, 2, 4, 5)
    k_pos = jnp.arange(S)

    def block(args):
        q_blk, start = args
        q_pos = start + jnp.arange(Q_BLOCK)
        bias = t5_table[:, t5_bucket(k_pos[None, :] - q_pos[:, None])].astype(jnp.float32)
        logits = jnp.einsum('bhmqd,bhmkd->bhmqk', q_blk, kh,
                            preferred_element_type=jnp.float32) * scale + bias[None, :, None]
        p = jax.nn.softmax(logits, axis=-1)
        attn = p[:, :, 0] - lam * p[:, :, 1]
        return jnp.einsum('bhqk,bhke->bhqe', attn.astype(vh.dtype), vh)

    out = lax.map(block, (qb, jnp.arange(n_blk) * Q_BLOCK))
    out = out.transpose(1, 0, 3, 2, 4).reshape(B, S, H, 2 * d)
    out = rms_norm(out, subln_g) * (1.0 - lam_init)
    return out.reshape(B, S, H * 2 * d)


def bidir_rglru(x, conv_w, conv_b, w_a, b_a, w_x, b_x, lam):
    B, S, W = x.shape
    xc = lax.conv_general_dilated(x, conv_w[:, None, :], window_strides=(1,), padding=[CONV_PAD],
                                  dimension_numbers=('NWC', 'WIO', 'NWC'),
                                  feature_group_count=W) + conv_b
    xb = xc.reshape(B, S, LRU_BLOCKS, LRU_BLOCK)
    za = jnp.einsum('bsnc,rncd->rbsnd', xb, w_a) + b_a.reshape(2, 1, 1, LRU_BLOCKS, LRU_BLOCK)
    zx = jnp.einsum('bsnc,rncd->rbsnd', xb, w_x) + b_x.reshape(2, 1, 1, LRU_BLOCKS, LRU_BLOCK)
    gate_a = jax.nn.sigmoid(za.astype(jnp.float32)).reshape(2, B, S, W)
    gate_x = jax.nn.sigmoid(zx.astype(jnp.float32)).reshape(2, B, S, W)
    log_a = -LRU_C * gate_a * jax.nn.softplus(-lam.astype(jnp.float32))[:, None, None, :]
    a = jnp.exp(log_a)
    mult = jnp.sqrt(-jnp.expm1(2.0 * log_a))
    u = mult * gate_x * xc.astype(jnp.float32)[None]

    def combine(l, r):
        return (l[0] * r[0], r[0] * l[1] + r[1])

    _, h_f = lax.associative_scan(combine, (a[0], u[0]), axis=1)
    _, h_b = lax.associative_scan(combine, (a[1], u[1]), axis=1, reverse=True)
    return (h_f + h_b).astype(x.dtype)


def neighborhood_attention(q, k, v, rpb):
    B, S, H, d = q.shape
    rows = S // GRID_W
    win_r = min(NA_WIN_R, rows)
    scale = d ** -0.5
    qg = q.reshape(B, rows, GRID_W, H, d).transpose(1, 0, 3, 2, 4)
    kg = k.reshape(B, rows, GRID_W, H, d).transpose(0, 3, 1, 2, 4)
    vg = v.reshape(B, rows, GRID_W, H, d).transpose(0, 3, 1, 2, 4)
    cols = jnp.arange(GRID_W)
    col_start = jnp.clip(cols - NA_WIN_C // 2, 0, GRID_W - NA_WIN_C)
    col_mask = (cols[None, :] >= col_start[:, None]) & (cols[None, :] < col_start[:, None] + NA_WIN_C)
    dc_idx = jnp.clip(cols[None, :] - cols[:, None], -(NA_WIN_C - 1), NA_WIN_C - 1) + NA_WIN_C - 1

    def row_block(args):
        q_row, r = args
        rs = jnp.clip(r - NA_WIN_R // 2, 0, rows - win_r)
        k_blk = lax.dynamic_slice_in_dim(kg, rs, win_r, axis=2)
        v_blk = lax.dynamic_slice_in_dim(vg, rs, win_r, axis=2)
        dr_idx = rs + jnp.arange(win_r) - r + NA_WIN_R - 1
        bias = rpb[:, dr_idx[None, :, None], dc_idx[:, None, :]].astype(jnp.float32)
        logits = jnp.einsum('bhqd,bhikd->bhqik', q_row, k_blk,
                            preferred_element_type=jnp.float32) * scale + bias
        logits = jnp.where(col_mask[:, None, :], logits, -jnp.inf)
        p = jax.nn.softmax(logits.reshape(B, H, GRID_W, win_r * GRID_W), axis=-1)
        p = p.reshape(B, H, GRID_W, win_r, GRID_W)
        return jnp.einsum('bhqik,bhikd->bhqd', p.astype(v_blk.dtype), v_blk)

    out = lax.map(row_block, (qg, jnp.arange(rows)))
    return out.transpose(1, 0, 3, 2, 4).reshape(B, S, H * d)


def setup_inputs(seed: int = 0) -> dict:
    key = jax.random.key(seed)
    ks = jax.random.split(key, 24)
    f32 = jnp.float32
    D = D_MODEL
    nrm = lambda k, shape, s: jax.random.normal(k, shape, f32) * s
    a0 = jax.random.uniform(ks[16], (N_EVEN, 2, LRU_WIDTH), f32, 0.9, 0.999)
    p0 = a0 ** (1.0 / LRU_C)
    lru_lambda = jnp.log(p0) - jnp.log1p(-p0)
    return {
        "x": nrm(ks[0], (BATCH, SEQ, D), 1.0),
        "c": nrm(ks[1], (BATCH, D), 1.0),
        "ada_w": nrm(ks[2], (DEPTH, D, 3 * D), 0.5 * D ** -0.5),
        "ada_b": nrm(ks[3], (DEPTH, 3 * D), 0.01),
        "norm_g": 1.0 + nrm(ks[4], (DEPTH, D), 0.02),
        "final_g": 1.0 + nrm(ks[5], (D,), 0.02),
        "t5_table": nrm(ks[6], (DA_HEADS, T5_BUCKETS), 0.5),
        "even_w_in": nrm(ks[7], (N_EVEN, D, EVEN_IN), D ** -0.5),
        "even_w_out": nrm(ks[8], (N_EVEN, EVEN_MIX, D), EVEN_MIX ** -0.5),
        "da_lam": nrm(ks[9], (N_EVEN, 4, DA_HEAD_DIM), 0.1),
        "da_subln_g": 1.0 + nrm(ks[10], (N_EVEN, 2 * DA_HEAD_DIM), 0.02),
        "lru_conv_w": nrm(ks[11], (N_EVEN, CONV_W, LRU_WIDTH), CONV_W ** -0.5),
        "lru_conv_b": nrm(ks[12], (N_EVEN, LRU_WIDTH), 0.01),
        "lru_w_a": nrm(ks[13], (N_EVEN, 2, LRU_BLOCKS, LRU_BLOCK, LRU_BLOCK), LRU_BLOCK ** -0.5),
        "lru_b_a": nrm(ks[14], (N_EVEN, 2, LRU_WIDTH), 0.01),
        "lru_w_x": nrm(ks[15], (N_EVEN, 2, LRU_BLOCKS, LRU_BLOCK, LRU_BLOCK), LRU_BLOCK ** -0.5),
        "lru_b_x": nrm(ks[17], (N_EVEN, 2, LRU_WIDTH), 0.01),
        "lru_lambda": lru_lambda,
        "odd_w_in": nrm(ks[18], (N_ODD, D, ODD_IN), D ** -0.5),
        "odd_w_out": nrm(ks[19], (N_ODD, NA_WIDTH, D), NA_WIDTH ** -0.5),
        "na_rpb": nrm(ks[20], (N_ODD, NA_HEADS, 2 * NA_WIN_R - 1, 2 * NA_WIN_C - 1), 0.3),
    }


def even_mixer(h, w_in, w_out, t5_table, lam_p, lam_init, subln_g,
               conv_w, conv_b, w_a, b_a, w_x, b_x, lru_lambda):
    B, S, _ = h.shape
    proj = h @ w_in
    q, k, v, g_a, x_b, g_b = jnp.split(
        proj, [DA_WIDTH, 2 * DA_WIDTH, 3 * DA_WIDTH, 4 * DA_WIDTH, 4 * DA_WIDTH + LRU_WIDTH], axis=-1)
    q = q.reshape(B, S, DA_HEADS, 2, DA_HEAD_DIM)
    k = k.reshape(B, S, DA_HEADS, 2, DA_HEAD_DIM)
    v = v.reshape(B, S, DA_HEADS, 2 * DA_HEAD_DIM)
    lp = lam_p.astype(jnp.float32)
    lam = jnp.exp(jnp.sum(lp[0] * lp[1])) - jnp.exp(jnp.sum(lp[2] * lp[3])) + lam_init
    a_out = diff_attention(q, k, v, t5_table, lam, lam_init, subln_g)
    b_out = bidir_rglru(x_b, conv_w, conv_b, w_a, b_a, w_x, b_x, lru_lambda)
    mixed = jnp.concatenate([a_out * jax.nn.silu(g_a), b_out * jax.nn.silu(g_b)], axis=-1)
    return mixed @ w_out


def odd_mixer(h, w_in, w_out, rpb):
    B, S, _ = h.shape
    proj = h @ w_in
    q, k, v, g = jnp.split(proj, 4, axis=-1)
    shp = (B, S, NA_HEADS, NA_HEAD_DIM)
    out = neighborhood_attention(q.reshape(shp), k.reshape(shp), v.reshape(shp), rpb)
    return (out * jax.nn.silu(g)) @ w_out


def reference(x, c, ada_w, ada_b, norm_g, final_g, t5_table, even_w_in, even_w_out, da_lam, da_subln_g,
              lru_conv_w, lru_conv_b, lru_w_a, lru_b_a, lru_w_x, lru_b_x, lru_lambda,
              odd_w_in, odd_w_out, na_rpb):
    c_act = jax.nn.silu(c)
    for l in range(DEPTH):
        mod = c_act @ ada_w[l] + ada_b[l]
        shift, scale, gate = jnp.split(mod, 3, axis=-1)
        h = rms_norm(x, norm_g[l]) * (1.0 + scale[:, None, :]) + shift[:, None, :]
        if l % 2 == 0:
            e = l // 2
            lam_init = 0.8 - 0.6 * math.exp(-0.3 * l)
            y = even_mixer(h, even_w_in[e], even_w_out[e], t5_table, da_lam[e], lam_init, da_subln_g[e],
                           lru_conv_w[e], lru_conv_b[e], lru_w_a[e], lru_b_a[e], lru_w_x[e], lru_b_x[e],
                           lru_lambda[e])
        else:
            o = l // 2
            y = odd_mixer(h, odd_w_in[o], odd_w_out[o], na_rpb[o])
        x = x + gate[:, None, :] * y
    return rms_norm(x, final_g)
```

```python
import math
import numpy as np
import concourse.bass as bass
import concourse.mybir as mybir
from concourse.bass_utils import run_bass_kernel_spmd

F32 = mybir.dt.float32
BF16 = mybir.dt.bfloat16
ALU = mybir.AluOpType
AF = mybir.ActivationFunctionType
AX = mybir.AxisListType

S = 2048
D = 1024
NB = 2
EPS = 1e-6
NEG = -30000.0
LAM_INIT = {0: 0.8 - 0.6 * math.exp(-0.3 * 0), 2: 0.8 - 0.6 * math.exp(-0.3 * 2)}


class Buf:
    __slots__ = ("name", "w", "r")

    def __init__(self, name):
        self.name = name
        self.w = None
        self.r = []


class Prog:
    ENGS = ["pe", "act", "dve", "pool", "sp"]
    SEM_LIMIT = 16000

    def __init__(self, nc):
        self.nc = nc
        self.ops = {e: [] for e in self.ENGS}
        self.cur = {}
        self.nsem = 0
        for e in self.ENGS:
            self._new_sem(e)
        self.dma_sems = {}
        self.dma_rr = {}
        self.pending = {e: [] for e in self.ENGS}
        self.dma_out = []

    def _new_sem(self, e):
        self.nsem += 1
        self.cur[e] = [self.nc.alloc_semaphore(name=f"s_{e}_{self.nsem}"), 0, e]

    def _tok(self, e):
        c = self.cur[e]
        if c[1] >= self.SEM_LIMIT:
            self._new_sem(e)
            c = self.cur[e]
        c[1] += 1
        return (c[0], c[1], e)

    @staticmethod
    def _deps(reads, writes, extra):
        waits = list(extra)
        for b in reads:
            if b.w is not None:
                waits.append(b.w)
        for b in writes:
            if b.w is not None:
                waits.append(b.w)
            waits.extend(b.r)
        return waits

    def op(self, eng, fn, reads=(), writes=(), extra=()):
        waits = self._deps(reads, writes, extra) + self.pending[eng]
        self.pending[eng] = []
        if eng == "pe":
            waits = [w for w in waits if w[2] != "pe"]
        tok = self._tok(eng)
        self.ops[eng].append((waits, fn, tok, 1))
        for b in reads:
            self._addr(b, tok)
        for b in writes:
            b.w = tok
            b.r = []
        return tok

    @staticmethod
    def _addr(b, tok):
        for i, t in enumerate(b.r):
            if t[0] is tok[0]:
                if tok[1] > t[1]:
                    b.r[i] = tok
                return
        b.r.append(tok)

    def dma(self, queue, parts, reads=(), writes=(), extra=()):
        waits = self._deps(reads, writes, extra) + self.pending[queue]
        self.pending[queue] = []
        NS = 8
        lst = self.dma_sems.setdefault(queue, [])
        if len(lst) < NS:
            self.nsem += 1
            s = [self.nc.alloc_semaphore(name=f"d_{queue}_{self.nsem}"), 0, None]
            lst.append(s)
        else:
            i = self.dma_rr.get(queue, 0)
            s = lst[i % NS]
            self.dma_rr[queue] = i + 1
            if s[1] > 30000:
                self.nsem += 1
                s2 = [self.nc.alloc_semaphore(name=f"d_{queue}_{self.nsem}"), 0, s[2]]
                lst[i % NS] = s2
                s = s2
        if s[2] is not None:
            waits.append(s[2])
        s[1] += 16 * len(parts)
        tok = (s[0], s[1], "dma")
        s[2] = tok

        def fn(eng, parts=parts, sem=s[0]):
            for (o, i_) in parts:
                eng.dma_start(out=o, in_=i_).then_inc(sem, 16)
            return None

        self.ops[queue].append((waits, fn, None, 0))
        for b in reads:
            self._addr(b, tok)
        for b in writes:
            b.w = tok
            b.r = []
        self.dma_out.append(tok)
        return tok

    def barrier(self):
        toks = []
        for e in self.ENGS:
            c = self.cur[e]
            if c[1] > 0:
                toks.append((c[0], c[1], e))
        toks.extend(self.dma_out)
        self.dma_out = []
        for e in self.ENGS:
            self.pending[e] = list(toks)

    def emit(self, final_waits):
        nc = self.nc
        engmap = {"pe": "tensor", "act": "scalar", "dve": "vector", "pool": "gpsimd", "sp": "sync"}
        with nc.Block() as block:
            for e in self.ENGS:
                ops = self.ops[e]

                def body(eng, ops=ops, e=e):
                    seen = {}
                    for (waits, fn, tok, inc) in ops:
                        for w in waits:
                            k = id(w[0])
                            if seen.get(k, 0) >= w[1]:
                                continue
                            eng.wait_ge(w[0], w[1])
                            seen[k] = w[1]
                        ins = fn(eng)
                        if inc:
                            ins.then_inc(tok[0], 1)
                    if e == "sp":
                        for w in final_waits:
                            eng.wait_ge(w[0], w[1])

                getattr(block, engmap[e])(body)


class Arena:
    def __init__(self, tensor, nwords):
        self.t = tensor
        self.n = nwords
        self.off = 0

    def reset(self):
        self.off = 0

    def alloc(self, shape, dt):
        per = int(np.prod(shape[1:]))
        words = per if dt == F32 else (per + 1) // 2
        words = (words + 15) // 16 * 16
        assert self.off + words <= self.n, f"arena overflow {self.off}+{words}>{self.n}"
        v = self.t[:, self.off:self.off + words]
        self.off += words
        if dt != F32:
            v = v.bitcast(dt)
        v = v[:, 0:per]
        if len(shape) == 3:
            v = v.rearrange("p (a b) -> p a b", a=shape[1])
        elif len(shape) == 4:
            v = v.rearrange("p (a b c) -> p a b c", a=shape[1], b=shape[2])
        return v


def _t5_bucket_np(rel):
    nb = 16
    max_exact = 8
    ret = np.where(rel > 0, nb, 0)
    n = np.abs(rel)
    large = np.full(n.shape, 15, dtype=np.int64)
    bounds = [8, 12, 16, 23, 32, 46, 64, 91, 128]
    for j in range(8):
        large = np.where((n >= bounds[j]) & (n < bounds[j + 1]), 8 + j, large)
    large = np.minimum(large, nb - 1)
    return ret + np.where(n < max_exact, n, large)


def _na_plan():
    rows = 32
    tiles = []
    plan = []
    for rp in range(16):
        r0 = 2 * rp
        win = []
        for b in range(2):
            r = r0 + b
            rs = min(max(r - 4, 0), rows - 8)
            win.append((rs, rs + 7))
        lo = min(w[0] for w in win) // 2
        hi = max(w[1] for w in win) // 2
        lst = []
        for j in range(lo, hi + 1):
            valid = tuple(tuple(int(win[b][0] <= 2 * j + a <= win[b][1]) for b in range(2)) for a in range(2))
            if not any(any(v) for v in valid):
                continue
            key = (2 * j - r0, valid)
            if key not in tiles:
                tiles.append(key)
            lst.append((j, tiles.index(key)))
        plan.append(lst)
    return plan, tiles


NA_PLAN, NA_TILES = _na_plan()
NT = len(NA_TILES)


def _na_bias_host(rpb):
    cols = np.arange(64)
    cs = np.clip(cols - 8, 0, 48)
    colmask = (cols[None, :] >= cs[:, None]) & (cols[None, :] < cs[:, None] + 16)
    dc = cols[None, :] - cols[:, None] + 15
    dc_cl = np.clip(dc, 0, 30)
    out = np.full((2, 16, NT, 128, 128), NEG, dtype=np.float32)
    for t, (Dd, valid) in enumerate(NA_TILES):
        for a in range(2):
            for b in range(2):
                if not valid[a][b]:
                    continue
                dr = Dd + a - b + 7
                assert 0 <= dr <= 14
                g = rpb[:, :, dr, :][:, :, dc_cl]
                g = np.where(colmask[None, None], g, NEG)
                out[:, :, t, a * 64:(a + 1) * 64, b * 64:(b + 1) * 64] = np.transpose(g, (0, 1, 3, 2))
    out = out.reshape(2, 8, 2, NT, 128, 128)
    out = np.transpose(out, (0, 1, 4, 2, 3, 5))
    return np.ascontiguousarray(out)


BTW = 896


def _t5_bias_host(t5_table):
    p = np.arange(128)[:, None]
    j = np.arange(BTW)[None, :]
    rel = p - j + 384
    bk = _t5_bucket_np(rel)
    return np.ascontiguousarray(t5_table[:, bk].astype(np.float32))


def build_program(nlayers=4):
    nc = bass.Bass("TRN2", target_bir_lowering=False)

    def din(name, shape):
        return nc.dram_tensor(name, list(shape), F32, kind="ExternalInput").ap()

    x_d = din("x", [NB, S, D])
    c_d = din("c_l", [128, 8, NB])
    adaw_d = din("ada_w", [4, D, 3 * D])
    adab_d = din("ada_b_l", [128, 4, 24])
    ng_d = din("norm_g_l", [128, 4, 8])
    fg_d = din("final_g_l", [128, 8])
    ewin_d = din("even_w_in", [2, D, 3072])
    ewout_d = din("even_w_out", [2, D, D])
    owin_d = din("odd_w_in", [2, D, 4096])
    owout_d = din("odd_w_out", [2, D, D])
    t5_d = din("t5bt", [4, 128, BTW])
    lam_d = din("da_lam_f", [1, 512])
    sg_d = din("subln_g_l", [128, 2])
    cw_d = din("conv_w_l", [128, 2, 4, 4])
    cb_d = din("conv_b_l", [128, 2, 4])
    wbd_d = din("lru_wbd", [2, 128, 16, 128])
    lb_d = din("lru_bias_l", [128, 2, 3, 2, 4])
    nab_d = din("na_bias", [2, 8, 128, 2, NT, 128])
    out_d = nc.dram_tensor("out", [NB, S, D], F32, kind="ExternalOutput").ap()

    P = Prog(nc)

    xT = nc.alloc_sbuf_tensor("xT", [128, 8, S], F32)
    hT = nc.alloc_sbuf_tensor("hT", [128, 8, S], BF16)
    mixT = nc.alloc_sbuf_tensor("mixT", [128, 8, S], BF16)
    ident32 = nc.alloc_sbuf_tensor("ident32", [128, 128], F32)
    identb = nc.alloc_sbuf_tensor("identb", [128, 128], BF16)
    ones32 = nc.alloc_sbuf_tensor("ones32", [128, 128], F32)
    cact = nc.alloc_sbuf_tensor("cact", [128, 8, NB], F32)
    modT = nc.alloc_sbuf_tensor("modT", [128, 4, 24, NB], F32)
    adab = nc.alloc_sbuf_tensor("adab", [128, 4, 24], F32)
    ng = nc.alloc_sbuf_tensor("ng", [128, 4, 8], F32)
    fg = nc.alloc_sbuf_tensor("fg", [128, 8], F32)
    scm = nc.alloc_sbuf_tensor("scm", [128, 4, NB, 8], F32)
    lamb = nc.alloc_sbuf_tensor("lamb", [128, 512], F32)
    lamt = nc.alloc_sbuf_tensor("lamt", [128, 128], F32)
    lamc = nc.alloc_sbuf_tensor("lamc", [128, 16], F32)
    sgl = nc.alloc_sbuf_tensor("sgl", [128, 2], F32)
    cw = nc.alloc_sbuf_tensor("cw", [128, 2, 4, 4], F32)
    cb = nc.alloc_sbuf_tensor("cb", [128, 2, 4], F32)
    lb = nc.alloc_sbuf_tensor("lb", [128, 2, 3, 2, 4], F32)
    nsp8 = nc.alloc_sbuf_tensor("nsp8", [128, 2, 2, 4], F32)
    rem_words = (nc.sbuf_bytes_remaining - 256) // 4
    rem_words = rem_words // 16 * 16
    arena_t = nc.alloc_sbuf_tensor("arena", [128, rem_words], F32)
    A = Arena(arena_t, rem_words)
    banks = [nc.alloc_psum_tensor(f"bank{i}", [128, 512], F32) for i in range(8)]
    bankB = [Buf(f"bank{i}") for i in range(8)]

    B_xT = [Buf(f"xT{i}") for i in range(4)]
    B_hT = [Buf(f"hT{i}") for i in range(4)]
    B_mix = [Buf(f"mix{i}") for i in range(8)]
    B_const = Buf("const")
    B_par = Buf("params")

    gen_rr = [0]

    def gen_bank():
        i = gen_rr[0] % 2
        gen_rr[0] += 1
        return banks[i], bankB[i]

    def tcs(tc):
        return slice(tc * 512, (tc + 1) * 512)

    def phase():
        P.barrier()
        A.reset()

    eps_col = lamc[:, 12:13]

    P.op("pool", lambda e: e.memset(ident32[:], 1.0), writes=[B_const])
    P.op("pool", lambda e: e.affine_select(out=ident32[:], in_=ident32[:], pattern=[[-1, 128]], compare_op=ALU.is_equal,
                                           fill=0.0, base=0, channel_multiplier=1), reads=[B_const], writes=[B_const])
    P.op("dve", lambda e: e.tensor_copy(out=identb[:], in_=ident32[:]), reads=[B_const], writes=[B_const])
    P.op("dve", lambda e: e.memset(ones32[:], 1.0), writes=[B_const])
    P.op("dve", lambda e: e.memset(lamc[:], 0.0), writes=[B_par])
    P.op("dve", lambda e: e.memset(eps_col, EPS), writes=[B_par])
    P.dma("sp", [(cact[:], c_d), (adab[:], adab_d), (ng[:], ng_d), (fg[:], fg_d), (sgl[:], sg_d), (cw[:], cw_d),
                 (cb[:], cb_d), (lb[:], lb_d), (lamb[:], lam_d.partition_broadcast(128))], writes=[B_par])
    P.op("act", lambda e: e.activation(out=cact[:], in_=cact[:], func=AF.Silu), reads=[B_par], writes=[B_par])

    wp_slots = [A.alloc([128, 8, 512], F32) for _ in range(2)]
    wp_B = [Buf("wp0"), Buf("wp1")]
    B_mod = Buf("mod")
    it = 0
    for l in range(nlayers):
        src = adaw_d[l].rearrange("(c p) n -> p c n", p=128)
        for piece in range(6):
            sl = it % 2
            it += 1
            wpt, wpb = wp_slots[sl], wp_B[sl]
            P.dma("sp", [(wpt[:, 0:4, :], src[:, 0:4, piece * 512:(piece + 1) * 512]),
                         (wpt[:, 4:8, :], src[:, 4:8, piece * 512:(piece + 1) * 512])], writes=[wpb])
            bk, bkB = gen_bank()
            for fc in range(4):
                for kc in range(8):
                    P.op("pe", lambda e, bk=bk, wpt=wpt, fc=fc, kc=kc: e.matmul(
                        bk[:, fc * NB:(fc + 1) * NB], lhsT=wpt[:, kc, fc * 128:(fc + 1) * 128], rhs=cact[:, kc, :],
                        start=(kc == 0), stop=(kc == 7)), reads=[wpb, B_par], writes=[bkB])
            P.op("dve", lambda e, bk=bk, l=l, piece=piece: e.tensor_tensor(
                out=modT[:, l, piece * 4:(piece + 1) * 4, :],
                in0=bk[:, 0:4 * NB].rearrange("p (a b) -> p a b", a=4),
                in1=adab[:, l, piece * 4:(piece + 1) * 4].unsqueeze(2).to_broadcast([128, 4, NB]),
                op=ALU.add), reads=[bkB, B_par], writes=[B_mod])
    for l in range(nlayers):
        for b in range(NB):
            P.op("dve", lambda e, l=l, b=b: e.scalar_tensor_tensor(
                out=scm[:, l, b, :], in0=modT[:, l, 8:16, b], scalar=1.0, in1=ng[:, l, :], op0=ALU.add, op1=ALU.mult),
                reads=[B_mod, B_par], writes=[B_mod])
    for e_ in range(2):
        for i in range(2):
            base = e_ * 256 + i * 128
            P.op("dve", lambda e, base=base: e.tensor_tensor(out=lamt[:, 0:64], in0=lamb[:, base:base + 64],
                                                             in1=lamb[:, base + 64:base + 128], op=ALU.mult),
                 reads=[B_par], writes=[B_par])
            P.op("dve", lambda e, e_=e_, i=i: e.reduce_sum(out=lamc[:, e_ * 2 + i:e_ * 2 + i + 1], in_=lamt[:, 0:64], axis=AX.X),
                 reads=[B_par], writes=[B_par])
    P.op("act", lambda e: e.activation(out=lamc[:, 4:8], in_=lamc[:, 0:4], func=AF.Exp), reads=[B_par], writes=[B_par])
    for e_ in range(2):
        li = LAM_INIT[2 * e_]
        P.op("dve", lambda e, e_=e_, li=li: e.scalar_tensor_tensor(
            out=lamc[:, 8 + e_:9 + e_], in0=lamc[:, 5 + 2 * e_:6 + 2 * e_], scalar=-li, in1=lamc[:, 4 + 2 * e_:5 + 2 * e_],
            op0=ALU.add, op1=ALU.subtract), reads=[B_par], writes=[B_par])
        P.op("dve", lambda e, e_=e_, li=li: e.tensor_scalar(
            out=lamc[:, 10 + e_:11 + e_], in0=sgl[:, e_:e_ + 1], scalar1=(1.0 - li), scalar2=None, op0=ALU.mult),
            reads=[B_par], writes=[B_par])
    P.op("act", lambda e: e.activation(out=nsp8[:], in_=lb[:, :, 2, :, :], func=AF.Exp, scale=-1.0), reads=[B_par], writes=[B_par])
    P.op("act", lambda e: e.activation(out=nsp8[:], in_=nsp8[:], func=AF.Ln, bias=1.0), reads=[B_par], writes=[B_par])
    P.op("dve", lambda e: e.tensor_scalar(out=nsp8[:], in0=nsp8[:], scalar1=-8.0, scalar2=None, op0=ALU.mult),
         reads=[B_par], writes=[B_par])
    B_par_all = [B_par, B_mod, B_const]

    def rms_stats(tc, tmp, reads):
        sq_slots, sqB, lnv, rstd, rB = tmp
        bk, bkB = gen_bank()
        for c in range(8):
            sl = c % len(sq_slots)
            P.op("act", lambda e, c=c, sl=sl: e.activation(out=sq_slots[sl], in_=xT[:, c, tcs(tc)], func=AF.Square),
                 reads=reads, writes=[sqB[sl]])
            P.op("pe", lambda e, c=c, sl=sl, bk=bk: e.matmul(bk[:, :], lhsT=ones32[:], rhs=sq_slots[sl], start=(c == 0), stop=(c == 7)),
                 reads=[sqB[sl], B_const], writes=[bkB])
        P.op("act", lambda e, bk=bk: e.activation(out=lnv, in_=bk[:, :], func=AF.Ln, scale=1.0 / D, bias=eps_col),
             reads=[bkB, B_par], writes=[rB])
        P.op("act", lambda e: e.activation(out=rstd, in_=lnv, func=AF.Exp, scale=-0.5), reads=[rB], writes=[rB])
        return rstd, rB

    def make_hT(l, b):
        sq_slots = [A.alloc([128, 512], F32) for _ in range(4)]
        sqB = [Buf(f"sq{i}") for i in range(4)]
        lnv = A.alloc([128, 512], F32)
        rstd = A.alloc([128, 512], F32)
        rB = Buf("rstd")
        t_slots = [A.alloc([128, 512], F32) for _ in range(2)]
        tB = [Buf("t0"), Buf("t1")]
        k = 0
        for tc in range(4):
            rstd_, rB_ = rms_stats(tc, (sq_slots, sqB, lnv, rstd, rB), [B_xT[tc]])
            for c in range(8):
                sl = k % 2
                k += 1
                P.op("dve", lambda e, c=c, sl=sl, tc=tc: e.scalar_tensor_tensor(
                    out=t_slots[sl], in0=xT[:, c, tcs(tc)], scalar=scm[:, l, b, c:c + 1], in1=rstd, op0=ALU.mult, op1=ALU.mult),
                    reads=[B_xT[tc], rB, B_mod], writes=[tB[sl]])
                P.op("act", lambda e, c=c, sl=sl, tc=tc: e.activation(
                    out=hT[:, c, tcs(tc)], in_=t_slots[sl], func=AF.Identity, bias=modT[:, l, c, b:b + 1], scale=1.0),
                    reads=[tB[sl], B_mod], writes=[B_hT[tc]])

    def load_w(dst, dstB, w2d, colbases, width):
        src = w2d.rearrange("(c p) n -> p c n", p=128)
        parts = [(dst[:, :, s_, :], src[:, :, cb_:cb_ + width]) for s_, cb_ in enumerate(colbases)]
        P.dma("pool", parts, writes=[dstB])

    def proj_feat(wsl, wB, s_, evac):
        for tc in range(4):
            bk, bkB = gen_bank()
            for c in range(8):
                P.op("pe", lambda e, c=c, bk=bk, tc=tc: e.matmul(bk[:, :], lhsT=wsl[:, c, s_, :], rhs=hT[:, c, tcs(tc)],
                                                                 start=(c == 0), stop=(c == 7)),
                     reads=[wB, B_hT[tc]], writes=[bkB])
            evac(tc, bk, bkB)

    def out_proj(l, b, w_out2d):
        phase()
        wout = A.alloc([128, 8, D], BF16)
        woB = Buf("wout")
        src = w_out2d.rearrange("(c p) n -> p c n", p=128)
        P.dma("pool", [(wout[:, 0:4, :], src[:, 0:4, :]), (wout[:, 4:8, :], src[:, 4:8, :])], writes=[woB])
        for tc in range(4):
            for fo in range(8):
                bk, bkB = gen_bank()
                for kc in range(8):
                    P.op("pe", lambda e, kc=kc, fo=fo, bk=bk, tc=tc: e.matmul(
                        bk[:, :], lhsT=wout[:, kc, fo * 128:(fo + 1) * 128], rhs=mixT[:, kc, tcs(tc)], start=(kc == 0), stop=(kc == 7)),
                        reads=[woB, B_mix[kc]], writes=[bkB])
                P.op("dve", lambda e, fo=fo, bk=bk, tc=tc: e.scalar_tensor_tensor(
                    out=xT[:, fo, tcs(tc)], in0=bk[:, :], scalar=modT[:, l, 16 + fo, b:b + 1], in1=xT[:, fo, tcs(tc)],
                    op0=ALU.mult, op1=ALU.add), reads=[bkB, B_mod], writes=[B_xT[tc]])

    def even_layer(l, b):
        e_ = l // 2
        w_in = ewin_d[e_]
        phase()
        make_hT(l, b)
        wsl = [A.alloc([128, 8, 4, 128], BF16) for _ in range(2)]
        wB = [Buf("wsl0"), Buf("wsl1")]
        qT = A.alloc([128, S], BF16)
        kT = A.alloc([128, S], BF16)
        gaT = A.alloc([128, S], BF16)
        v = A.alloc([128, 16, 129], BF16)
        bt = A.alloc([128, BTW], BF16)
        Bq, Bk, Bga, Bv, Bbt = Buf("q"), Buf("k"), Buf("ga"), Buf("v"), Buf("bt")
        Es = [A.alloc([128, 512], BF16) for _ in range(2)]
        EB = [Buf("E0"), Buf("E1")]
        o0s = [A.alloc([128, 128], F32) for _ in range(2)]
        ds = [A.alloc([128, 128], F32) for _ in range(2)]
        sqd = A.alloc([128, 128], F32)
        dns = [A.alloc([128, 128], BF16) for _ in range(2)]
        small = A.alloc([128, 16], F32)
        Bpost = [Buf("post0"), Buf("post1")]
        Bsm = Buf("small")
        neglam = lamc[:, 8 + e_:9 + e_]
        sgc = lamc[:, 10 + e_:11 + e_]
        for h in range(4):
            sl = h % 2
            load_w(wsl[sl], wB[sl], w_in, [h * 128, 512 + h * 128, 1024 + h * 128, 1536 + h * 128], 128)
            P.dma("pool", [(bt[:, :], t5_d[h])], writes=[Bbt])
            P.op("dve", lambda e: e.memset(v[:, :, 128:129], 1.0), writes=[Bv])

            def ev_q(tc, bk, bkB):
                P.op("dve", lambda e: e.tensor_scalar(out=qT[:, tcs(tc)], in0=bk[:, :], scalar1=0.125, scalar2=None, op0=ALU.mult),
                     reads=[bkB], writes=[Bq])

            def ev_k(tc, bk, bkB):
                P.op("dve", lambda e: e.tensor_copy(out=kT[:, tcs(tc)], in_=bk[:, :]), reads=[bkB], writes=[Bk])

            def ev_g(tc, bk, bkB):
                P.op("act", lambda e: e.activation(out=gaT[:, tcs(tc)], in_=bk[:, :], func=AF.Silu), reads=[bkB], writes=[Bga])

            proj_feat(wsl[sl], wB[sl], 0, ev_q)
            proj_feat(wsl[sl], wB[sl], 1, ev_k)
            proj_feat(wsl[sl], wB[sl], 3, ev_g)
            for g4 in range(4):
                bk, bkB = gen_bank()
                for i in range(4):
                    tt = g4 * 4 + i
                    for c in range(8):
                        P.op("pe", lambda e, c=c, bk=bk, i=i, tt=tt, sl=sl: e.matmul(
                            bk[:, i * 128:(i + 1) * 128], lhsT=hT[:, c, tt * 128:(tt + 1) * 128], rhs=wsl[sl][:, c, 2, :],
                            start=(c == 0), stop=(c == 7)), reads=[wB[sl], B_hT[tt // 4]], writes=[bkB])
                P.op("act", lambda e, bk=bk, g4=g4: e.activation(
                    out=v[:, g4 * 4:(g4 + 1) * 4, 0:128], in_=bk[:, :].rearrange("p (a b) -> p a b", a=4), func=AF.Copy),
                    reads=[bkB], writes=[Bv])
            for qc in range(8):
                Ob = [(banks[4], bankB[4], banks[5], bankB[5]), (banks[6], bankB[6], banks[7], bankB[7])][qc % 2]
                O = [Ob[0], Ob[2]]
                OB = [Ob[1], Ob[3]]

                def qk(kc, qc=qc):
                    si = kc % 2
                    Sb, SB = banks[2 + si], bankB[2 + si]
                    Dd = kc * 128 - qc * 256
                    Dc = max(-256, min(384, Dd))
                    off = 384 - Dc
                    for m in range(2):
                        P.op("pe", lambda e, m=m, Sb=Sb, kc=kc: e.matmul(
                            Sb[:, m * 256:(m + 1) * 256], lhsT=kT[m * 64:(m + 1) * 64, kc * 128:(kc + 1) * 128],
                            rhs=qT[m * 64:(m + 1) * 64, qc * 256:(qc + 1) * 256], start=True, stop=False),
                            reads=[Bq, Bk], writes=[SB])
                        P.op("pe", lambda e, m=m, Sb=Sb, off=off: e.matmul(
                            Sb[:, m * 256:(m + 1) * 256], lhsT=identb[:], rhs=bt[:, off:off + 256], start=False, stop=True),
                            reads=[Bbt, B_const], writes=[SB])
                    P.op("act", lambda e, Sb=Sb, si=si: e.activation(out=Es[si], in_=Sb[:, :], func=AF.Exp),
                         reads=[SB], writes=[EB[si]])

                def av(kc):
                    si = kc % 2
                    for m in range(2):
                        for qs in range(2):
                            P.op("pe", lambda e, m=m, qs=qs, si=si, kc=kc, O=O: e.matmul(
                                O[m][:, qs * 129:(qs + 1) * 129], lhsT=Es[si][:, m * 256 + qs * 128:m * 256 + (qs + 1) * 128],
                                rhs=v[:, kc, :], start=(kc == 0 and qs == 0), stop=(kc == 15), skip_group_check=True),
                                reads=[EB[si], Bv], writes=[OB[m]])

                qk(0)
                for kc in range(16):
                    if kc + 1 < 16:
                        qk(kc + 1)
                    av(kc)
                O3 = [O[m][:, 0:258].rearrange("p (a b) -> p a b", a=2) for m in range(2)]
                for m in range(2):
                    P.op("dve", lambda e, m=m, O3=O3: e.reciprocal(out=small[:, m * 2:(m + 1) * 2], in_=O3[m][:, :, 128]),
                         reads=[OB[m]], writes=[Bsm])
                P.op("dve", lambda e: e.tensor_scalar(out=small[:, 4:6], in0=small[:, 2:4], scalar1=neglam, scalar2=None, op0=ALU.mult),
                     reads=[Bsm, B_par], writes=[Bsm])
                for qs in range(2):
                    pi = qs
                    qb = qc * 2 + qs
                    P.op("dve", lambda e, qs=qs, pi=pi, O3=O3: e.tensor_scalar(
                        out=o0s[pi], in0=O3[0][:, qs, 0:128], scalar1=small[:, qs:qs + 1], scalar2=None, op0=ALU.mult),
                        reads=[OB[0], Bsm], writes=[Bpost[pi]])
                    P.op("dve", lambda e, qs=qs, pi=pi, O3=O3: e.scalar_tensor_tensor(
                        out=ds[pi], in0=O3[1][:, qs, 0:128], scalar=small[:, 4 + qs:5 + qs], in1=o0s[pi], op0=ALU.mult, op1=ALU.add),
                        reads=[OB[1], Bsm, Bpost[pi]], writes=[Bpost[pi]])
                    P.op("dve", lambda e, pi=pi: e.tensor_tensor(out=sqd, in0=ds[pi], in1=ds[pi], op=ALU.mult),
                         reads=[Bpost[pi]], writes=[Bsm])
                    P.op("dve", lambda e: e.reduce_sum(out=small[:, 6:7], in_=sqd, axis=AX.X), reads=[Bsm], writes=[Bsm])
                    P.op("act", lambda e: e.activation(out=small[:, 7:8], in_=small[:, 6:7], func=AF.Ln, scale=1.0 / 128, bias=eps_col),
                         reads=[Bsm, B_par], writes=[Bsm])
                    P.op("act", lambda e: e.activation(out=small[:, 8:9], in_=small[:, 7:8], func=AF.Exp, scale=-0.5),
                         reads=[Bsm], writes=[Bsm])
                    P.op("dve", lambda e, pi=pi: e.tensor_scalar(out=dns[pi], in0=ds[pi], scalar1=small[:, 8:9], scalar2=None, op0=ALU.mult),
                         reads=[Bpost[pi], Bsm], writes=[Bpost[pi]])
                    bk, bkB = gen_bank()
                    bkb = bk[:, :].bitcast(BF16)
                    P.op("pe", lambda e, pi=pi, bkb=bkb: e.transpose(out=bkb[:, 0:128], in_=dns[pi], identity=identb[:]),
                         reads=[Bpost[pi], B_const], writes=[bkB])
                    P.op("dve", lambda e, bkb=bkb, qb=qb, h=h: e.scalar_tensor_tensor(
                        out=mixT[:, h, qb * 128:(qb + 1) * 128], in0=bkb[:, 0:128], scalar=sgc, in1=gaT[:, qb * 128:(qb + 1) * 128],
                        op0=ALU.mult, op1=ALU.mult), reads=[bkB, Bga, B_par], writes=[B_mix[h]])

        phase()
        wsl2 = A.alloc([128, 8, 2, 128], BF16)
        w2B = Buf("wsl2")
        wbd = A.alloc([128, 16, 128], BF16)
        wbdB = Buf("wbd")
        P.dma("pool", [(wbd[:, :, :], wbd_d[e_])], writes=[wbdB])
        xb = A.alloc([128, S], F32)
        xc = A.alloc([128, S], F32)
        xcb = A.alloc([128, S], BF16)
        gb = A.alloc([128, S], BF16)
        T2 = A.alloc([128, S], F32)
        T3 = A.alloc([128, S], F32)
        T4 = A.alloc([128, S], F32)
        Bxb, Bxc, Bxcb, Bgb, BT2, BT3, BT4 = (Buf(n) for n in ["xb", "xc", "xcb", "gb", "T2", "T3", "T4"])

        def rev(ap2d):
            (ps_, pn_), (fs_, fn_) = ap2d.ap
            return bass.AP(ap2d.tensor, ap2d.offset + (fn_ - 1) * fs_, [[ps_, pn_], [-fs_, fn_]])

        for c in range(4):
            load_w(wsl2, w2B, w_in, [2048 + c * 128, 2560 + c * 128], 128)

            def ev_xb(tc, bk, bkB):
                P.op("dve", lambda e: e.tensor_copy(out=xb[:, tcs(tc)], in_=bk[:, :]), reads=[bkB], writes=[Bxb])

            def ev_gb(tc, bk, bkB):
                P.op("act", lambda e: e.activation(out=gb[:, tcs(tc)], in_=bk[:, :], func=AF.Silu), reads=[bkB], writes=[Bgb])

            proj_feat(wsl2, w2B, 0, ev_xb)
            proj_feat(wsl2, w2B, 1, ev_gb)
            w0, w1, w2, w3 = (cw[:, e_, c, j:j + 1] for j in range(4))
            cbc = cb[:, e_, c:c + 1]
            P.op("dve", lambda e, w2=w2, cbc=cbc: e.tensor_scalar(out=xc[:, :], in0=xb[:, :], scalar1=w2, scalar2=cbc,
                                                                  op0=ALU.mult, op1=ALU.add), reads=[Bxb, B_par], writes=[Bxc])
            P.op("dve", lambda e, w0=w0: e.scalar_tensor_tensor(out=xc[:, 2:S], in0=xb[:, 0:S - 2], scalar=w0, in1=xc[:, 2:S],
                                                                op0=ALU.mult, op1=ALU.add), reads=[Bxb, B_par, Bxc], writes=[Bxc])
            P.op("dve", lambda e, w1=w1: e.scalar_tensor_tensor(out=xc[:, 1:S], in0=xb[:, 0:S - 1], scalar=w1, in1=xc[:, 1:S],
                                                                op0=ALU.mult, op1=ALU.add), reads=[Bxb, B_par, Bxc], writes=[Bxc])
            P.op("dve", lambda e, w3=w3: e.scalar_tensor_tensor(out=xc[:, 0:S - 1], in0=xb[:, 1:S], scalar=w3, in1=xc[:, 0:S - 1],
                                                                op0=ALU.mult, op1=ALU.add), reads=[Bxb, B_par, Bxc], writes=[Bxc])
            P.op("act", lambda e: e.activation(out=xcb[:, :], in_=xc[:, :], func=AF.Copy), reads=[Bxc], writes=[Bxcb])
            for r in range(2):
                Aa, BA = xb, Bxb
                M_, BM = T2, BT2
                X_, BX = (T3, BT3) if r == 0 else (T4, BT4)
                for tc in range(4):
                    for g_, dst, dB, brow in ((0, Aa, BA, 0), (1, X_, BX, 1)):
                        bk, bkB = gen_bank()
                        P.op("pe", lambda e, bk=bk, g_=g_, r=r, c=c, tc=tc: e.matmul(
                            bk[:, :], lhsT=wbd[:, (g_ * 2 + r) * 4 + c, :], rhs=xcb[:, tcs(tc)], start=True, stop=True),
                            reads=[wbdB, Bxcb], writes=[bkB])
                        P.op("act", lambda e, bk=bk, dst=dst, brow=brow, r=r, c=c, tc=tc: e.activation(
                            out=dst[:, tcs(tc)], in_=bk[:, :], func=AF.Sigmoid, bias=lb[:, e_, brow, r, c:c + 1], scale=1.0),
                            reads=[bkB, B_par], writes=[dB])
                P.op("act", lambda e, r=r, c=c: e.activation(out=Aa[:, :], in_=Aa[:, :], func=AF.Exp, scale=nsp8[:, e_, r, c:c + 1]),
                     reads=[BA, B_par], writes=[BA])
                P.op("dve", lambda e: e.tensor_tensor(out=M_[:, :], in0=Aa[:, :], in1=Aa[:, :], op=ALU.mult), reads=[BA], writes=[BM])
                P.op("dve", lambda e: e.tensor_scalar(out=M_[:, :], in0=M_[:, :], scalar1=-1.0, scalar2=1.0, op0=ALU.mult, op1=ALU.add),
                     reads=[BM], writes=[BM])
                P.op("act", lambda e: e.activation(out=M_[:, :], in_=M_[:, :], func=AF.Sqrt), reads=[BM], writes=[BM])
                P.op("dve", lambda e, X_=X_: e.tensor_tensor(out=M_[:, :], in0=M_[:, :], in1=X_[:, :], op=ALU.mult), reads=[BM, BX], writes=[BM])
                P.op("dve", lambda e: e.tensor_tensor(out=M_[:, :], in0=M_[:, :], in1=xc[:, :], op=ALU.mult), reads=[BM, Bxc], writes=[BM])
                if r == 0:
                    P.op("dve", lambda e, X_=X_: e.tensor_tensor_scan(out=X_[:, :], data0=Aa[:, :], data1=M_[:, :], initial=0.0,
                                                                      op0=ALU.mult, op1=ALU.add), reads=[BA, BM], writes=[BX])
                else:
                    P.op("dve", lambda e, X_=X_: e.tensor_tensor_scan(out=rev(X_[:, :]), data0=rev(Aa[:, :]), data1=rev(M_[:, :]), initial=0.0,
                                                                      op0=ALU.mult, op1=ALU.add), reads=[BA, BM], writes=[BX])
            P.op("dve", lambda e: e.tensor_tensor(out=T3[:, :], in0=T3[:, :], in1=T4[:, :], op=ALU.add), reads=[BT3, BT4], writes=[BT3])
            P.op("dve", lambda e, c=c: e.tensor_tensor(out=mixT[:, 4 + c, :], in0=T3[:, :], in1=gb[:, :], op=ALU.mult),
                 reads=[BT3, Bgb], writes=[B_mix[4 + c]])
        out_proj(l, b, ewout_d[e_])

    def odd_layer(l, b):
        o_ = l // 2
        w_in = owin_d[o_]
        phase()
        make_hT(l, b)
        wsl = [A.alloc([128, 8, 4, 128], BF16) for _ in range(2)]
        wB = [Buf("wsl0"), Buf("wsl1")]
        qT = A.alloc([128, S], BF16)
        kT = A.alloc([128, S], BF16)
        gT = A.alloc([128, S], BF16)
        v = A.alloc([128, 16, 2, 65], BF16)
        nb = A.alloc([128, 2, NT, 128], BF16)
        Bq, Bk, Bg, Bv, Bnb = Buf("q"), Buf("k"), Buf("g"), Buf("v"), Buf("nb")
        Es = [A.alloc([128, 512], BF16) for _ in range(2)]
        EB = [Buf("E0"), Buf("E1")]
        obf = [A.alloc([128, 128], BF16) for _ in range(2)]
        Bo = [Buf("o0"), Buf("o1")]
        small = A.alloc([128, 4], F32)
        Bsm = Buf("small")
        for hp in range(8):
            sl = hp % 2
            load_w(wsl[sl], wB[sl], w_in, [hp * 128, 1024 + hp * 128, 2048 + hp * 128, 3072 + hp * 128], 128)
            P.dma("pool", [(nb[:, :, :, :], nab_d[o_, hp])], writes=[Bnb])
            P.op("dve", lambda e: e.memset(v[:, :, :, 64:65], 1.0), writes=[Bv])

            def ev_q(tc, bk, bkB):
                P.op("dve", lambda e: e.tensor_scalar(out=qT[:, tcs(tc)], in0=bk[:, :], scalar1=0.125, scalar2=None, op0=ALU.mult),
                     reads=[bkB], writes=[Bq])

            def ev_k(tc, bk, bkB):
                P.op("dve", lambda e: e.tensor_copy(out=kT[:, tcs(tc)], in_=bk[:, :]), reads=[bkB], writes=[Bk])

            def ev_g(tc, bk, bkB):
                P.op("act", lambda e: e.activation(out=gT[:, tcs(tc)], in_=bk[:, :], func=AF.Silu), reads=[bkB], writes=[Bg])

            proj_feat(wsl[sl], wB[sl], 0, ev_q)
            proj_feat(wsl[sl], wB[sl], 1, ev_k)
            proj_feat(wsl[sl], wB[sl], 3, ev_g)
            for g4 in range(4):
                bk, bkB = gen_bank()
                for i in range(4):
                    tt = g4 * 4 + i
                    for c in range(8):
                        P.op("pe", lambda e, c=c, bk=bk, i=i, tt=tt, sl=sl: e.matmul(
                            bk[:, i * 128:(i + 1) * 128], lhsT=hT[:, c, tt * 128:(tt + 1) * 128], rhs=wsl[sl][:, c, 2, :],
                            start=(c == 0), stop=(c == 7)), reads=[wB[sl], B_hT[tt // 4]], writes=[bkB])
                P.op("act", lambda e, bk=bk, g4=g4: e.activation(
                    out=v[:, g4 * 4:(g4 + 1) * 4, :, 0:64], in_=bk[:, :].rearrange("p (a b c) -> p a b c", a=4, b=2), func=AF.Copy),
                    reads=[bkB], writes=[Bv])
            work = []
            for rp in range(16):
                ch = NA_PLAN[rp]
                groups = [ch[i:i + 2] for i in range(0, len(ch), 2)]
                for gi, grp in enumerate(groups):
                    work.append((rp, gi, len(groups), grp))
            state = {"n": 0}

            def qk(w):
                rp, gi, ng_, grp = w
                si = state["n"] % 2
                state["n"] += 1
                Sb, SB = banks[2 + si], bankB[2 + si]
                for ci, (j, t) in enumerate(grp):
                    for hh in range(2):
                        col = (ci * 2 + hh) * 128
                        P.op("pe", lambda e, Sb=Sb, col=col, hh=hh, j=j, rp=rp: e.matmul(
                            Sb[:, col:col + 128], lhsT=kT[hh * 64:(hh + 1) * 64, j * 128:(j + 1) * 128],
                            rhs=qT[hh * 64:(hh + 1) * 64, rp * 128:(rp + 1) * 128], start=True, stop=False),
                            reads=[Bq, Bk], writes=[SB])
                        P.op("pe", lambda e, Sb=Sb, col=col, hh=hh, t=t: e.matmul(
                            Sb[:, col:col + 128], lhsT=identb[:], rhs=nb[:, hh, t, :], start=False, stop=True),
                            reads=[Bnb, B_const], writes=[SB])
                ncol = len(grp) * 256
                P.op("act", lambda e, Sb=Sb, si=si, ncol=ncol: e.activation(out=Es[si][:, 0:ncol], in_=Sb[:, 0:ncol], func=AF.Exp),
                     reads=[SB], writes=[EB[si]])
                return si

            def av(w, si):
                rp, gi, ng_, grp = w
                Ob, OBf = banks[4 + rp % 4], bankB[4 + rp % 4]
                for ci, (j, t) in enumerate(grp):
                    for hh in range(2):
                        col = (ci * 2 + hh) * 128
                        first = (gi == 0 and ci == 0)
                        last = (gi == ng_ - 1 and ci == len(grp) - 1)
                        P.op("pe", lambda e, Ob=Ob, col=col, hh=hh, j=j, si=si, first=first, last=last: e.matmul(
                            Ob[:, hh * 65:(hh + 1) * 65], lhsT=Es[si][:, col:col + 128], rhs=v[:, j, hh, :],
                            start=(first and hh == 0), stop=last, skip_group_check=True),
                            reads=[EB[si], Bv], writes=[OBf])
                if gi == ng_ - 1:
                    post(rp, Ob, OBf)

            def post(rp, Ob, OBf, hp=hp):
                pi = rp % 2
                O3 = Ob[:, 0:130].rearrange("p (a b) -> p a b", a=2)
                P.op("dve", lambda e: e.reciprocal(out=small[:, 0:2], in_=O3[:, :, 64]), reads=[OBf], writes=[Bsm])
                for hh in range(2):
                    P.op("dve", lambda e, hh=hh: e.tensor_scalar(out=obf[pi][:, hh * 64:(hh + 1) * 64], in0=O3[:, hh, 0:64],
                                                                 scalar1=small[:, hh:hh + 1], scalar2=None, op0=ALU.mult),
                         reads=[OBf, Bsm], writes=[Bo[pi]])
                bk, bkB = gen_bank()
                bkb = bk[:, :].bitcast(BF16)
                P.op("pe", lambda e: e.transpose(out=bkb[:, 0:128], in_=obf[pi], identity=identb[:]), reads=[Bo[pi], B_const], writes=[bkB])
                P.op("dve", lambda e: e.tensor_tensor(out=mixT[:, hp, rp * 128:(rp + 1) * 128], in0=bkb[:, 0:128],
                                                      in1=gT[:, rp * 128:(rp + 1) * 128], op=ALU.mult),
                     reads=[bkB, Bg], writes=[B_mix[hp]])

            si_cur = qk(work[0])
            for wi in range(len(work)):
                si_next = None
                if wi + 1 < len(work):
                    si_next = qk(work[wi + 1])
                av(work[wi], si_cur)
                si_cur = si_next
        out_proj(l, b, owout_d[o_])

    final_toks = []
    for b in range(NB):
        phase()
        xtok = [A.alloc([128, D], F32) for _ in range(3)]
        xB = [Buf(f"xtok{i}") for i in range(3)]
        for tt in range(16):
            sl = tt % 3
            P.dma("sp", [(xtok[sl][:, :], x_d[b, tt * 128:(tt + 1) * 128, :])], writes=[xB[sl]])
            for half in range(2):
                bk, bkB = gen_bank()
                for i in range(4):
                    c = half * 4 + i
                    P.op("pe", lambda e, bk=bk, i=i, c=c, sl=sl: e.transpose(out=bk[:, i * 128:(i + 1) * 128],
                                                                             in_=xtok[sl][:, c * 128:(c + 1) * 128], identity=ident32[:]),
                         reads=[xB[sl], B_const], writes=[bkB])
                eng = "dve" if half == 0 else "act"
                if eng == "dve":
                    P.op("dve", lambda e, bk=bk, half=half, tt=tt: e.tensor_copy(
                        out=xT[:, half * 4:(half + 1) * 4, tt * 128:(tt + 1) * 128], in_=bk[:, :].rearrange("p (a b) -> p a b", a=4)),
                        reads=[bkB], writes=[B_xT[tt // 4]])
                else:
                    P.op("act", lambda e, bk=bk, half=half, tt=tt: e.activation(
                        out=xT[:, half * 4:(half + 1) * 4, tt * 128:(tt + 1) * 128], in_=bk[:, :].rearrange("p (a b) -> p a b", a=4), func=AF.Copy),
                        reads=[bkB], writes=[B_xT[tt // 4]])
        for l in range(nlayers):
            if l % 2 == 0:
                even_layer(l, b)
            else:
                odd_layer(l, b)
        phase()
        sq_slots = [A.alloc([128, 512], F32) for _ in range(4)]
        sqB = [Buf(f"sq{i}") for i in range(4)]
        lnv = A.alloc([128, 512], F32)
        rstd = A.alloc([128, 512], F32)
        rB = Buf("rstd")
        yT = A.alloc([128, 8, 512], F32)
        yB = Buf("yT")
        otok = [A.alloc([128, D], F32) for _ in range(2)]
        oB = [Buf("otok0"), Buf("otok1")]
        k = 0
        for tc in range(4):
            rms_stats(tc, (sq_slots, sqB, lnv, rstd, rB), [B_xT[tc]])
            for c in range(8):
                P.op("dve", lambda e, c=c, tc=tc: e.scalar_tensor_tensor(
                    out=yT[:, c, :], in0=xT[:, c, tcs(tc)], scalar=fg[:, c:c + 1], in1=rstd, op0=ALU.mult, op1=ALU.mult),
                    reads=[B_xT[tc], rB, B_par], writes=[yB])
            for i4 in range(4):
                tt = tc * 4 + i4
                sl = k % 2
                k += 1
                for half in range(2):
                    bk, bkB = gen_bank()
                    for i in range(4):
                        c = half * 4 + i
                        P.op("pe", lambda e, bk=bk, i=i, c=c, i4=i4: e.transpose(
                            out=bk[:, i * 128:(i + 1) * 128], in_=yT[:, c, i4 * 128:(i4 + 1) * 128], identity=ident32[:]),
                            reads=[yB, B_const], writes=[bkB])
                    if half == 0:
                        P.op("dve", lambda e, bk=bk, sl=sl, half=half: e.tensor_copy(out=otok[sl][:, half * 512:(half + 1) * 512], in_=bk[:, :]),
                             reads=[bkB], writes=[oB[sl]])
                    else:
                        P.op("act", lambda e, bk=bk, sl=sl, half=half: e.activation(out=otok[sl][:, half * 512:(half + 1) * 512], in_=bk[:, :], func=AF.Copy),
                             reads=[bkB], writes=[oB[sl]])
                final_toks.append(P.dma("sp", [(out_d[b, tt * 128:(tt + 1) * 128, :], otok[sl][:, :])], reads=[oB[sl]]))
    P.emit(final_toks)
    return nc


_NC_CACHE = {}


def _prep_shared(inp):
    f = lambda a: np.ascontiguousarray(np.asarray(a, dtype=np.float32))
    sh = {}
    sh["ada_w"] = f(inp["ada_w"])
    sh["ada_b_l"] = f(np.transpose(np.asarray(inp["ada_b"]).reshape(4, 24, 128), (2, 0, 1)))
    sh["norm_g_l"] = f(np.transpose(np.asarray(inp["norm_g"]).reshape(4, 8, 128), (2, 0, 1)))
    sh["final_g_l"] = f(np.asarray(inp["final_g"]).reshape(8, 128).T)
    sh["even_w_in"] = f(inp["even_w_in"])
    sh["even_w_out"] = f(inp["even_w_out"])
    sh["odd_w_in"] = f(inp["odd_w_in"])
    sh["odd_w_out"] = f(inp["odd_w_out"])
    sh["t5bt"] = _t5_bias_host(np.asarray(inp["t5_table"], dtype=np.float32))
    sh["da_lam_f"] = f(np.asarray(inp["da_lam"]).reshape(1, 512))
    sh["subln_g_l"] = f(np.asarray(inp["da_subln_g"]).T)
    sh["conv_w_l"] = f(np.transpose(np.asarray(inp["lru_conv_w"]).reshape(2, 4, 4, 128), (3, 0, 2, 1)))
    sh["conv_b_l"] = f(np.transpose(np.asarray(inp["lru_conv_b"]).reshape(2, 4, 128), (2, 0, 1)))
    wa = np.asarray(inp["lru_w_a"], dtype=np.float32)
    wx = np.asarray(inp["lru_w_x"], dtype=np.float32)
    wbd = np.zeros((2, 128, 2, 2, 4, 128), dtype=np.float32)
    for g_, w in enumerate((wa, wx)):
        for c in range(4):
            for half in range(2):
                blk = w[:, :, 2 * c + half]
                wbd[:, half * 64:(half + 1) * 64, g_, :, c, half * 64:(half + 1) * 64] = np.transpose(blk, (0, 2, 1, 3))
    sh["lru_wbd"] = np.ascontiguousarray(wbd.reshape(2, 128, 16, 128))
    lbias = np.stack([np.asarray(inp["lru_b_a"]), np.asarray(inp["lru_b_x"]), np.asarray(inp["lru_lambda"])], axis=1)
    sh["lru_bias_l"] = f(np.transpose(lbias.reshape(2, 3, 2, 4, 128), (4, 0, 1, 2, 3)))
    sh["na_bias"] = _na_bias_host(np.asarray(inp["na_rpb"], dtype=np.float32))
    return sh


def kernel(nlayers=4, cores=8, **inp):
    x = np.asarray(inp["x"], dtype=np.float32)
    c = np.asarray(inp["c"], dtype=np.float32)
    sh = _prep_shared(inp)
    key = nlayers
    if key not in _NC_CACHE:
        _NC_CACHE[key] = build_program(nlayers)
    nc = _NC_CACHE[key]
    in_maps = []
    for i in range(cores):
        m = dict(sh)
        m["x"] = np.ascontiguousarray(x[NB * i:NB * (i + 1)])
        cc = c[NB * i:NB * (i + 1)]
        m["c_l"] = np.ascontiguousarray(np.transpose(cc.T.reshape(8, 128, NB), (1, 0, 2)))
        in_maps.append(m)
    res = run_bass_kernel_spmd(nc, in_maps, core_ids=list(range(cores)))
    out = np.concatenate([np.asarray(r["out"], dtype=np.float32) for r in res.results], axis=0)
    return out
```

```python
import math
import numpy as np
import concourse.bass as bass
import concourse.mybir as mybir
from concourse.bass_utils import run_bass_kernel_spmd

F32 = mybir.dt.float32
BF16 = mybir.dt.bfloat16
ALU = mybir.AluOpType
AF = mybir.ActivationFunctionType
AX = mybir.AxisListType

S = 2048
D = 1024
NB = 2
EPS = 1e-6
NEG = -30000.0
LAM_INIT = {0: 0.8 - 0.6 * math.exp(-0.3 * 0), 2: 0.8 - 0.6 * math.exp(-0.3 * 2)}


class Buf:
    __slots__ = ("name", "w", "r")

    def __init__(self, name):
        self.name = name
        self.w = None
        self.r = []


class Prog:
    ENGS = ["pe", "act", "dve", "pool", "sp"]
    SEM_LIMIT = 16000

    def __init__(self, nc):
        self.nc = nc
        self.ops = {e: [] for e in self.ENGS}
        self.cur = {}
        self.nsem = 0
        for e in self.ENGS:
            self._new_sem(e)
        self.dma_sems = {}
        self.dma_rr = {}
        self.pending = {e: [] for e in self.ENGS}
        self.dma_out = []

    def _new_sem(self, e):
        self.nsem += 1
        self.cur[e] = [self.nc.alloc_semaphore(name=f"s_{e}_{self.nsem}"), 0, e]

    def _tok(self, e):
        c = self.cur[e]
        if c[1] >= self.SEM_LIMIT:
            self._new_sem(e)
            c = self.cur[e]
        c[1] += 1
        return (c[0], c[1], e)

    @staticmethod
    def _deps(reads, writes, extra):
        waits = list(extra)
        for b in reads:
            if b.w is not None:
                waits.append(b.w)
        for b in writes:
            if b.w is not None:
                waits.append(b.w)
            waits.extend(b.r)
        return waits

    def op(self, eng, fn, reads=(), writes=(), extra=()):
        waits = self._deps(reads, writes, extra) + self.pending[eng]
        self.pending[eng] = []
        if eng == "pe":
            waits = [w for w in waits if w[2] != "pe"]
        tok = self._tok(eng)
        self.ops[eng].append((waits, fn, tok, 1))
        for b in reads:
            self._addr(b, tok)
        for b in writes:
            b.w = tok
            b.r = []
        return tok

    @staticmethod
    def _addr(b, tok):
        for i, t in enumerate(b.r):
            if t[0] is tok[0]:
                if tok[1] > t[1]:
                    b.r[i] = tok
                return
        b.r.append(tok)

    def dma(self, queue, parts, reads=(), writes=(), extra=()):
        waits = self._deps(reads, writes, extra) + self.pending[queue]
        self.pending[queue] = []
        NS = 8
        lst = self.dma_sems.setdefault(queue, [])
        if len(lst) < NS:
            self.nsem += 1
            s = [self.nc.alloc_semaphore(name=f"d_{queue}_{self.nsem}"), 0, None]
            lst.append(s)
        else:
            i = self.dma_rr.get(queue, 0)
            s = lst[i % NS]
            self.dma_rr[queue] = i + 1
            if s[1] > 30000:
                self.nsem += 1
                s2 = [self.nc.alloc_semaphore(name=f"d_{queue}_{self.nsem}"), 0, s[2]]
                lst[i % NS] = s2
                s = s2
        if s[2] is not None:
            waits.append(s[2])
        s[1] += 16 * len(parts)
        tok = (s[0], s[1], "dma")
        s[2] = tok

        def fn(eng, parts=parts, sem=s[0]):
            for (o, i_) in parts:
                eng.dma_start(out=o, in_=i_).then_inc(sem, 16)
            return None

        self.ops[queue].append((waits, fn, None, 0))
        for b in reads:
            self._addr(b, tok)
        for b in writes:
            b.w = tok
            b.r = []
        self.dma_out.append(tok)
        return tok

    def barrier(self):
        toks = []
        for e in self.ENGS:
            c = self.cur[e]
            if c[1] > 0:
                toks.append((c[0], c[1], e))
        toks.extend(self.dma_out)
        self.dma_out = []
        for e in self.ENGS:
            self.pending[e] = list(toks)

    def emit(self, final_waits):
        nc = self.nc
        engmap = {"pe": "tensor", "act": "scalar", "dve": "vector", "pool": "gpsimd", "sp": "sync"}
        with nc.Block() as block:
            for e in self.ENGS:
                ops = self.ops[e]

                def body(eng, ops=ops, e=e):
                    seen = {}
                    for (waits, fn, tok, inc) in ops:
                        for w in waits:
                            k = id(w[0])
                            if seen.get(k, 0) >= w[1]:
                                continue
                            eng.wait_ge(w[0], w[1])
                            seen[k] = w[1]
                        ins = fn(eng)
                        if inc:
                            ins.then_inc(tok[0], 1)
                    if e == "sp":
                        for w in final_waits:
                            eng.wait_ge(w[0], w[1])

                getattr(block, engmap[e])(body)


class Arena:
    def __init__(self, tensor, nwords):
        self.t = tensor
        self.n = nwords
        self.off = 0

    def reset(self):
        self.off = 0

    def alloc(self, shape, dt):
        per = int(np.prod(shape[1:]))
        words = per if dt == F32 else (per + 1) // 2
        words = (words + 15) // 16 * 16
        assert self.off + words <= self.n, f"arena overflow {self.off}+{words}>{self.n}"
        v = self.t[:, self.off:self.off + words]
        self.off += words
        if dt != F32:
            v = v.bitcast(dt)
        v = v[:, 0:per]
        if len(shape) == 3:
            v = v.rearrange("p (a b) -> p a b", a=shape[1])
        elif len(shape) == 4:
            v = v.rearrange("p (a b c) -> p a b c", a=shape[1], b=shape[2])
        return v


def _t5_bucket_np(rel):
    nb = 16
    max_exact = 8
    ret = np.where(rel > 0, nb, 0)
    n = np.abs(rel)
    large = np.full(n.shape, 15, dtype=np.int64)
    bounds = [8, 12, 16, 23, 32, 46, 64, 91, 128]
    for j in range(8):
        large = np.where((n >= bounds[j]) & (n < bounds[j + 1]), 8 + j, large)
    large = np.minimum(large, nb - 1)
    return ret + np.where(n < max_exact, n, large)


def _na_plan():
    rows = 32
    tiles = []
    plan = []
    for rp in range(16):
        r0 = 2 * rp
        win = []
        for b in range(2):
            r = r0 + b
            rs = min(max(r - 4, 0), rows - 8)
            win.append((rs, rs + 7))
        lo = min(w[0] for w in win) // 2
        hi = max(w[1] for w in win) // 2
        lst = []
        for j in range(lo, hi + 1):
            valid = tuple(tuple(int(win[b][0] <= 2 * j + a <= win[b][1]) for b in range(2)) for a in range(2))
            if not any(any(v) for v in valid):
                continue
            key = (2 * j - r0, valid)
            if key not in tiles:
                tiles.append(key)
            lst.append((j, tiles.index(key)))
        plan.append(lst)
    return plan, tiles


NA_PLAN, NA_TILES = _na_plan()
NT = len(NA_TILES)


def _na_bias_host(rpb):
    cols = np.arange(64)
    cs = np.clip(cols - 8, 0, 48)
    colmask = (cols[None, :] >= cs[:, None]) & (cols[None, :] < cs[:, None] + 16)
    dc = cols[None, :] - cols[:, None] + 15
    dc_cl = np.clip(dc, 0, 30)
    out = np.full((2, 16, NT, 128, 128), NEG, dtype=np.float32)
    for t, (Dd, valid) in enumerate(NA_TILES):
        for a in range(2):
            for b in range(2):
                if not valid[a][b]:
                    continue
                dr = Dd + a - b + 7
                assert 0 <= dr <= 14
                g = rpb[:, :, dr, :][:, :, dc_cl]
                g = np.where(colmask[None, None], g, NEG)
                out[:, :, t, a * 64:(a + 1) * 64, b * 64:(b + 1) * 64] = np.transpose(g, (0, 1, 3, 2))
    out = out.reshape(2, 8, 2, NT, 128, 128)
    out = np.transpose(out, (0, 1, 4, 2, 3, 5))
    return np.ascontiguousarray(out)


BTW = 896


def _t5_bias_host(t5_table):
    p = np.arange(128)[:, None]
    j = np.arange(BTW)[None, :]
    rel = p - j + 384
    bk = _t5_bucket_np(rel)
    return np.ascontiguousarray(t5_table[:, bk].astype(np.float32))


def build_program(nlayers=4):
    nc = bass.Bass("TRN2", target_bir_lowering=False)

    def din(name, shape):
        return nc.dram_tensor(name, list(shape), F32, kind="ExternalInput").ap()

    x_d = din("x", [NB, S, D])
    c_d = din("c_l", [128, 8, NB])
    adaw_d = din("ada_w", [4, D, 3 * D])
    adab_d = din("ada_b_l", [128, 4, 24])
    ng_d = din("norm_g_l", [128, 4, 8])
    fg_d = din("final_g_l", [128, 8])
    ewin_d = din("even_w_in", [2, D, 3072])
    ewout_d = din("even_w_out", [2, D, D])
    owin_d = din("odd_w_in", [2, D, 4096])
    owout_d = din("odd_w_out", [2, D, D])
    t5_d = din("t5bt", [4, 128, BTW])
    lam_d = din("da_lam_f", [1, 512])
    sg_d = din("subln_g_l", [128, 2])
    cw_d = din("conv_w_l", [128, 2, 4, 4])
    cb_d = din("conv_b_l", [128, 2, 4])
    wbd_d = din("lru_wbd", [2, 128, 16, 128])
    lb_d = din("lru_bias_l", [128, 2, 3, 2, 4])
    nab_d = din("na_bias", [2, 8, 128, 2, NT, 128])
    out_d = nc.dram_tensor("out", [NB, S, D], F32, kind="ExternalOutput").ap()

    P = Prog(nc)

    xT = nc.alloc_sbuf_tensor("xT", [128, 8, S], F32)
    hT = nc.alloc_sbuf_tensor("hT", [128, 8, S], BF16)
    mixT = nc.alloc_sbuf_tensor("mixT", [128, 8, S], BF16)
    ident32 = nc.alloc_sbuf_tensor("ident32", [128, 128], F32)
    identb = nc.alloc_sbuf_tensor("identb", [128, 128], BF16)
    ones32 = nc.alloc_sbuf_tensor("ones32", [128, 128], F32)
    onesb = nc.alloc_sbuf_tensor("onesb", [128, 128], BF16)
    cact = nc.alloc_sbuf_tensor("cact", [128, 8, NB], F32)
    modT = nc.alloc_sbuf_tensor("modT", [128, 4, 24, NB], F32)
    adab = nc.alloc_sbuf_tensor("adab", [128, 4, 24], F32)
    ng = nc.alloc_sbuf_tensor("ng", [128, 4, 8], F32)
    fg = nc.alloc_sbuf_tensor("fg", [128, 8], F32)
    scm = nc.alloc_sbuf_tensor("scm", [128, 4, NB, 8], F32)
    lamb = nc.alloc_sbuf_tensor("lamb", [128, 512], F32)
    lamt = nc.alloc_sbuf_tensor("lamt", [128, 128], F32)
    lamc = nc.alloc_sbuf_tensor("lamc", [128, 16], F32)
    sgl = nc.alloc_sbuf_tensor("sgl", [128, 2], F32)
    cw = nc.alloc_sbuf_tensor("cw", [128, 2, 4, 4], F32)
    cb = nc.alloc_sbuf_tensor("cb", [128, 2, 4], F32)
    lb = nc.alloc_sbuf_tensor("lb", [128, 2, 3, 2, 4], F32)
    nsp8 = nc.alloc_sbuf_tensor("nsp8", [128, 2, 2, 4], F32)
    rem_words = (nc.sbuf_bytes_remaining - 256) // 4
    rem_words = rem_words // 16 * 16
    arena_t = nc.alloc_sbuf_tensor("arena", [128, rem_words], F32)
    A = Arena(arena_t, rem_words)
    banks = [nc.alloc_psum_tensor(f"bank{i}", [128, 512], F32) for i in range(8)]
    bankB = [Buf(f"bank{i}") for i in range(8)]

    B_xT = [Buf(f"xT{i}") for i in range(4)]
    B_hT = [Buf(f"hT{i}") for i in range(4)]
    B_mix = [Buf(f"mix{i}") for i in range(8)]
    B_const = Buf("const")
    B_par = Buf("params")

    gen_rr = [0]

    def gen_bank():
        i = gen_rr[0] % 2
        gen_rr[0] += 1
        return banks[i], bankB[i]

    def tcs(tc):
        return slice(tc * 512, (tc + 1) * 512)

    def phase():
        P.barrier()
        A.reset()

    eps_col = lamc[:, 12:13]

    P.op("pool", lambda e: e.memset(ident32[:], 1.0), writes=[B_const])
    P.op("pool", lambda e: e.affine_select(out=ident32[:], in_=ident32[:], pattern=[[-1, 128]], compare_op=ALU.is_equal,
                                           fill=0.0, base=0, channel_multiplier=1), reads=[B_const], writes=[B_const])
    P.op("dve", lambda e: e.tensor_copy(out=identb[:], in_=ident32[:]), reads=[B_const], writes=[B_const])
    P.op("dve", lambda e: e.memset(ones32[:], 1.0), writes=[B_const])
    P.op("dve", lambda e: e.memset(onesb[:], 1.0), writes=[B_const])
    P.op("dve", lambda e: e.memset(lamc[:], 0.0), writes=[B_par])
    P.op("dve", lambda e: e.memset(eps_col, EPS), writes=[B_par])
    P.dma("sp", [(cact[:], c_d), (adab[:], adab_d), (ng[:], ng_d), (fg[:], fg_d), (sgl[:], sg_d), (cw[:], cw_d),
                 (cb[:], cb_d), (lb[:], lb_d), (lamb[:], lam_d.partition_broadcast(128))], writes=[B_par])
    P.op("act", lambda e: e.activation(out=cact[:], in_=cact[:], func=AF.Silu), reads=[B_par], writes=[B_par])

    wp_slots = [A.alloc([128, 8, 512], F32) for _ in range(2)]
    wp_B = [Buf("wp0"), Buf("wp1")]
    B_mod = Buf("mod")
    it = 0
    for l in range(nlayers):
        src = adaw_d[l].rearrange("(c p) n -> p c n", p=128)
        for piece in range(6):
            sl = it % 2
            it += 1
            wpt, wpb = wp_slots[sl], wp_B[sl]
            P.dma("sp", [(wpt[:, 0:4, :], src[:, 0:4, piece * 512:(piece + 1) * 512]),
                         (wpt[:, 4:8, :], src[:, 4:8, piece * 512:(piece + 1) * 512])], writes=[wpb])
            bk, bkB = gen_bank()
            for fc in range(4):
                for kc in range(8):
                    P.op("pe", lambda e, bk=bk, wpt=wpt, fc=fc, kc=kc: e.matmul(
                        bk[:, fc * NB:(fc + 1) * NB], lhsT=wpt[:, kc, fc * 128:(fc + 1) * 128], rhs=cact[:, kc, :],
                        start=(kc == 0), stop=(kc == 7)), reads=[wpb, B_par], writes=[bkB])
            P.op("dve", lambda e, bk=bk, l=l, piece=piece: e.tensor_tensor(
                out=modT[:, l, piece * 4:(piece + 1) * 4, :],
                in0=bk[:, 0:4 * NB].rearrange("p (a b) -> p a b", a=4),
                in1=adab[:, l, piece * 4:(piece + 1) * 4].unsqueeze(2).to_broadcast([128, 4, NB]),
                op=ALU.add), reads=[bkB, B_par], writes=[B_mod])
    for l in range(nlayers):
        for b in range(NB):
            P.op("dve", lambda e, l=l, b=b: e.scalar_tensor_tensor(
                out=scm[:, l, b, :], in0=modT[:, l, 8:16, b], scalar=1.0, in1=ng[:, l, :], op0=ALU.add, op1=ALU.mult),
                reads=[B_mod, B_par], writes=[B_mod])
    for e_ in range(2):
        for i in range(2):
            base = e_ * 256 + i * 128
            P.op("dve", lambda e, base=base: e.tensor_tensor(out=lamt[:, 0:64], in0=lamb[:, base:base + 64],
                                                             in1=lamb[:, base + 64:base + 128], op=ALU.mult),
                 reads=[B_par], writes=[B_par])
            P.op("dve", lambda e, e_=e_, i=i: e.reduce_sum(out=lamc[:, e_ * 2 + i:e_ * 2 + i + 1], in_=lamt[:, 0:64], axis=AX.X),
                 reads=[B_par], writes=[B_par])
    P.op("act", lambda e: e.activation(out=lamc[:, 4:8], in_=lamc[:, 0:4], func=AF.Exp), reads=[B_par], writes=[B_par])
    for e_ in range(2):
        li = LAM_INIT[2 * e_]
        P.op("dve", lambda e, e_=e_, li=li: e.scalar_tensor_tensor(
            out=lamc[:, 8 + e_:9 + e_], in0=lamc[:, 5 + 2 * e_:6 + 2 * e_], scalar=-li, in1=lamc[:, 4 + 2 * e_:5 + 2 * e_],
            op0=ALU.add, op1=ALU.subtract), reads=[B_par], writes=[B_par])
        P.op("dve", lambda e, e_=e_, li=li: e.tensor_scalar(
            out=lamc[:, 10 + e_:11 + e_], in0=sgl[:, e_:e_ + 1], scalar1=(1.0 - li), scalar2=None, op0=ALU.mult),
            reads=[B_par], writes=[B_par])
    P.op("act", lambda e: e.activation(out=nsp8[:], in_=lb[:, :, 2, :, :], func=AF.Exp, scale=-1.0), reads=[B_par], writes=[B_par])
    P.op("act", lambda e: e.activation(out=nsp8[:], in_=nsp8[:], func=AF.Ln, bias=1.0), reads=[B_par], writes=[B_par])
    P.op("dve", lambda e: e.tensor_scalar(out=nsp8[:], in0=nsp8[:], scalar1=-8.0, scalar2=None, op0=ALU.mult),
         reads=[B_par], writes=[B_par])
    B_par_all = [B_par, B_mod, B_const]

    def rms_stats(tc, tmp, reads):
        sq_slots, sqB, lnv, rstd, rB = tmp
        bk, bkB = gen_bank()
        for c in range(8):
            sl = c % len(sq_slots)
            P.op("act", lambda e, c=c, sl=sl: e.activation(out=sq_slots[sl], in_=xT[:, c, tcs(tc)], func=AF.Square),
                 reads=reads, writes=[sqB[sl]])
            P.op("pe", lambda e, c=c, sl=sl, bk=bk: e.matmul(bk[:, :], lhsT=onesb[:], rhs=sq_slots[sl], start=(c == 0), stop=(c == 7)),
                 reads=[sqB[sl], B_const], writes=[bkB])
        P.op("act", lambda e, bk=bk: e.activation(out=lnv, in_=bk[:, :], func=AF.Ln, scale=1.0 / D, bias=eps_col),
             reads=[bkB, B_par], writes=[rB])
        P.op("act", lambda e: e.activation(out=rstd, in_=lnv, func=AF.Exp, scale=-0.5), reads=[rB], writes=[rB])
        return rstd, rB

    def make_hT(l, b):
        sq_slots = [A.alloc([128, 512], BF16) for _ in range(4)]
        sqB = [Buf(f"sq{i}") for i in range(4)]
        lnv = A.alloc([128, 512], F32)
        rstd = A.alloc([128, 512], F32)
        rB = Buf("rstd")
        t_slots = [A.alloc([128, 512], F32) for _ in range(2)]
        tB = [Buf("t0"), Buf("t1")]
        k = 0
        for tc in range(4):
            rstd_, rB_ = rms_stats(tc, (sq_slots, sqB, lnv, rstd, rB), [B_xT[tc]])
            for c in range(8):
                sl = k % 2
                k += 1
                P.op("dve", lambda e, c=c, sl=sl, tc=tc: e.scalar_tensor_tensor(
                    out=t_slots[sl], in0=xT[:, c, tcs(tc)], scalar=scm[:, l, b, c:c + 1], in1=rstd, op0=ALU.mult, op1=ALU.mult),
                    reads=[B_xT[tc], rB, B_mod], writes=[tB[sl]])
                P.op("act", lambda e, c=c, sl=sl, tc=tc: e.activation(
                    out=hT[:, c, tcs(tc)], in_=t_slots[sl], func=AF.Identity, bias=modT[:, l, c, b:b + 1], scale=1.0),
                    reads=[tB[sl], B_mod], writes=[B_hT[tc]])

    def load_w(dst, dstB, w2d, colbases, width):
        src = w2d.rearrange("(c p) n -> p c n", p=128)
        parts = [(dst[:, :, s_, :], src[:, :, cb_:cb_ + width]) for s_, cb_ in enumerate(colbases)]
        P.dma("pool", parts, writes=[dstB])

    def proj_feat(wsl, wB, s_, evac):
        for tc in range(4):
            bk, bkB = gen_bank()
            for c in range(8):
                P.op("pe", lambda e, c=c, bk=bk, tc=tc: e.matmul(bk[:, :], lhsT=wsl[:, c, s_, :], rhs=hT[:, c, tcs(tc)],
                                                                 start=(c == 0), stop=(c == 7)),
                     reads=[wB, B_hT[tc]], writes=[bkB])
            evac(tc, bk, bkB)

    def out_proj(l, b, w_out2d):
        phase()
        wout = A.alloc([128, 8, D], BF16)
        woB = Buf("wout")
        src = w_out2d.rearrange("(c p) n -> p c n", p=128)
        P.dma("pool", [(wout[:, 0:4, :], src[:, 0:4, :]), (wout[:, 4:8, :], src[:, 4:8, :])], writes=[woB])
        for tc in range(4):
            for fo in range(8):
                bk, bkB = gen_bank()
                for kc in range(8):
                    P.op("pe", lambda e, kc=kc, fo=fo, bk=bk, tc=tc: e.matmul(
                        bk[:, :], lhsT=wout[:, kc, fo * 128:(fo + 1) * 128], rhs=mixT[:, kc, tcs(tc)], start=(kc == 0), stop=(kc == 7)),
                        reads=[woB, B_mix[kc]], writes=[bkB])
                P.op("dve", lambda e, fo=fo, bk=bk, tc=tc: e.scalar_tensor_tensor(
                    out=xT[:, fo, tcs(tc)], in0=bk[:, :], scalar=modT[:, l, 16 + fo, b:b + 1], in1=xT[:, fo, tcs(tc)],
                    op0=ALU.mult, op1=ALU.add), reads=[bkB, B_mod], writes=[B_xT[tc]])

    def even_layer(l, b):
        e_ = l // 2
        w_in = ewin_d[e_]
        phase()
        make_hT(l, b)
        phase()
        wsl = [A.alloc([128, 8, 4, 128], BF16) for _ in range(2)]
        wB = [Buf("wsl0"), Buf("wsl1")]
        qT2 = A.alloc([128, 2, S], BF16)
        kT = A.alloc([128, S], BF16)
        gaT = A.alloc([128, S], BF16)
        v = A.alloc([128, 16, 129], BF16)
        bt32 = A.alloc([128, BTW], F32)
        ebt = A.alloc([128, BTW], BF16)
        Bq, Bk, Bga, Bv, Bbt = Buf("q"), Buf("k"), Buf("ga"), Buf("v"), Buf("bt")
        NE = 3
        Es = [A.alloc([128, 512], BF16) for _ in range(NE)]
        EB = [Buf(f"E{i}") for i in range(NE)]
        P.op("dve", lambda e: e.memset(qT2[64:128, 0, :], 0.0), writes=[Bq])
        P.op("dve", lambda e: e.memset(qT2[0:64, 1, :], 0.0), writes=[Bq])
        o0s = [A.alloc([128, 128], F32) for _ in range(2)]
        ds = [A.alloc([128, 128], F32) for _ in range(2)]
        sqd = A.alloc([128, 128], F32)
        dns = [A.alloc([128, 128], BF16) for _ in range(2)]
        small = A.alloc([128, 16], F32)
        Bpost = [Buf("post0"), Buf("post1")]
        Bsm = Buf("small")
        neglam = lamc[:, 8 + e_:9 + e_]
        sgc = lamc[:, 10 + e_:11 + e_]
        for h in range(4):
            sl = h % 2
            load_w(wsl[sl], wB[sl], w_in, [h * 128, 512 + h * 128, 1024 + h * 128, 1536 + h * 128], 128)
            P.dma("sp", [(bt32[:, :], t5_d[h])], writes=[Bbt])
            P.op("act", lambda e: e.activation(out=ebt[:, :], in_=bt32[:, :], func=AF.Exp), reads=[Bbt], writes=[Bbt])
            P.op("dve", lambda e: e.memset(v[:, :, 128:129], 1.0), writes=[Bv])
            pend = []

            def ev_q(tc, bk, bkB):
                for m_ in range(2):
                    P.op("dve", lambda e, m_=m_: e.tensor_scalar(out=qT2[m_ * 64:(m_ + 1) * 64, m_, tcs(tc)], in0=bk[m_ * 64:(m_ + 1) * 64, :],
                                                                 scalar1=0.125, scalar2=None, op0=ALU.mult),
                         reads=[bkB], writes=[Bq])

            def ev_k(tc, bk, bkB):
                P.op("dve", lambda e: e.tensor_copy(out=kT[:, tcs(tc)], in_=bk[:, :]), reads=[bkB], writes=[Bk])

            def ev_g(tc, bk, bkB):
                P.op("act", lambda e: e.activation(out=gaT[:, tcs(tc)], in_=bk[:, :], func=AF.Silu), reads=[bkB], writes=[Bga])

            proj_feat(wsl[sl], wB[sl], 0, ev_q)
            proj_feat(wsl[sl], wB[sl], 1, ev_k)
            proj_feat(wsl[sl], wB[sl], 3, ev_g)
            for g4 in range(4):
                bk, bkB = gen_bank()
                for i in range(4):
                    tt = g4 * 4 + i
                    for c in range(8):
                        P.op("pe", lambda e, c=c, bk=bk, i=i, tt=tt, sl=sl: e.matmul(
                            bk[:, i * 128:(i + 1) * 128], lhsT=hT[:, c, tt * 128:(tt + 1) * 128], rhs=wsl[sl][:, c, 2, :],
                            start=(c == 0), stop=(c == 7)), reads=[wB[sl], B_hT[tt // 4]], writes=[bkB])
                P.op("act", lambda e, bk=bk, g4=g4: e.activation(
                    out=v[:, g4 * 4:(g4 + 1) * 4, 0:128], in_=bk[:, :].rearrange("p (a b) -> p a b", a=4), func=AF.Copy),
                    reads=[bkB], writes=[Bv])
            for qc in range(8):
                Ob = [(banks[4], bankB[4], banks[5], bankB[5]), (banks[6], bankB[6], banks[7], bankB[7])][qc % 2]
                O = [Ob[0], Ob[2]]
                OB = [Ob[1], Ob[3]]

                def qk(kc, qc=qc):
                    si = kc % 2
                    ei = kc % NE
                    Sb, SB = banks[2 + si], bankB[2 + si]
                    Dd = kc * 128 - qc * 256
                    for m in range(2):
                        P.op("pe", lambda e, m=m, Sb=Sb, kc=kc: e.matmul(
                            Sb[:, m * 256:(m + 1) * 256], lhsT=kT[:, kc * 128:(kc + 1) * 128],
                            rhs=qT2[:, m, qc * 256:(qc + 1) * 256], start=True, stop=True),
                            reads=[Bq, Bk], writes=[SB])
                    if Dd >= 384 or Dd <= -256:
                        cb_ = bt32[:, 0:1] if Dd >= 384 else bt32[:, BTW - 1:BTW]
                        P.op("act", lambda e, Sb=Sb, ei=ei, cb_=cb_: e.activation(out=Es[ei], in_=Sb[:, :], func=AF.Exp, bias=cb_, scale=1.0),
                             reads=[SB, Bbt], writes=[EB[ei]])
                    else:
                        off = 384 - Dd
                        P.op("act", lambda e, Sb=Sb, ei=ei: e.activation(out=Es[ei], in_=Sb[:, :], func=AF.Exp),
                             reads=[SB], writes=[EB[ei]])
                        P.op("dve", lambda e, ei=ei, off=off: e.tensor_tensor(
                            out=Es[ei].rearrange("p (m q) -> p m q", m=2), in0=Es[ei].rearrange("p (m q) -> p m q", m=2),
                            in1=ebt[:, off:off + 256].unsqueeze(1).to_broadcast([128, 2, 256]), op=ALU.mult),
                            reads=[Bbt], writes=[EB[ei]])

                def av(kc):
                    si = kc % NE
                    for m in range(2):
                        for qs in range(2):
                            P.op("pe", lambda e, m=m, qs=qs, si=si, kc=kc, O=O: e.matmul(
                                O[m][:, qs * 129:(qs + 1) * 129], lhsT=Es[si][:, m * 256 + qs * 128:m * 256 + (qs + 1) * 128],
                                rhs=v[:, kc, :], start=(kc == 0 and qs == 0), stop=(kc == 15), skip_group_check=True),
                                reads=[EB[si], Bv], writes=[OB[m]])

                qk(0)
                for kc in range(16):
                    if kc + 1 < 16:
                        qk(kc + 1)
                    av(kc)
                    if kc == 5:
                        for f_ in pend:
                            f_()
                        pend.clear()
                O3 = [O[m][:, 0:258].rearrange("p (a b) -> p a b", a=2) for m in range(2)]
                for m in range(2):
                    P.op("dve", lambda e, m=m, O3=O3: e.reciprocal(out=small[:, m * 2:(m + 1) * 2], in_=O3[m][:, :, 128]),
                         reads=[OB[m]], writes=[Bsm])
                P.op("dve", lambda e: e.tensor_scalar(out=small[:, 4:6], in0=small[:, 2:4], scalar1=neglam, scalar2=None, op0=ALU.mult),
                     reads=[Bsm, B_par], writes=[Bsm])
                for qs in range(2):
                    pi = qs
                    qb = qc * 2 + qs
                    P.op("dve", lambda e, qs=qs, pi=pi, O3=O3: e.tensor_scalar(
                        out=o0s[pi], in0=O3[0][:, qs, 0:128], scalar1=small[:, qs:qs + 1], scalar2=None, op0=ALU.mult),
                        reads=[OB[0], Bsm], writes=[Bpost[pi]])
                    P.op("dve", lambda e, qs=qs, pi=pi, O3=O3: e.scalar_tensor_tensor(
                        out=ds[pi], in0=O3[1][:, qs, 0:128], scalar=small[:, 4 + qs:5 + qs], in1=o0s[pi], op0=ALU.mult, op1=ALU.add),
                        reads=[OB[1], Bsm, Bpost[pi]], writes=[Bpost[pi]])
                    P.op("dve", lambda e, pi=pi: e.tensor_tensor(out=sqd, in0=ds[pi], in1=ds[pi], op=ALU.mult),
                         reads=[Bpost[pi]], writes=[Bsm])
                    P.op("dve", lambda e: e.reduce_sum(out=small[:, 6:7], in_=sqd, axis=AX.X), reads=[Bsm], writes=[Bsm])
                    P.op("act", lambda e: e.activation(out=small[:, 7:8], in_=small[:, 6:7], func=AF.Ln, scale=1.0 / 128, bias=eps_col),
                         reads=[Bsm, B_par], writes=[Bsm])
                    P.op("act", lambda e: e.activation(out=small[:, 8:9], in_=small[:, 7:8], func=AF.Exp, scale=-0.5),
                         reads=[Bsm], writes=[Bsm])
                    P.op("dve", lambda e, pi=pi: e.tensor_scalar(out=dns[pi], in0=ds[pi], scalar1=small[:, 8:9], scalar2=None, op0=ALU.mult),
                         reads=[Bpost[pi], Bsm], writes=[Bpost[pi]])

                    def post_b(pi=pi, qb=qb, h=h):
                        bk, bkB = gen_bank()
                        bkb = bk[:, :].bitcast(BF16)
                        P.op("pe", lambda e, pi=pi, bkb=bkb: e.transpose(out=bkb[:, 0:128], in_=dns[pi], identity=identb[:]),
                             reads=[Bpost[pi], B_const], writes=[bkB])
                        P.op("dve", lambda e, bkb=bkb, qb=qb, h=h: e.scalar_tensor_tensor(
                            out=mixT[:, h, qb * 128:(qb + 1) * 128], in0=bkb[:, 0:128], scalar=sgc, in1=gaT[:, qb * 128:(qb + 1) * 128],
                            op0=ALU.mult, op1=ALU.mult), reads=[bkB, Bga, B_par], writes=[B_mix[h]])
                    pend.append(post_b)
            for f_ in pend:
                f_()
            pend.clear()

        phase()
        wsl2 = A.alloc([128, 8, 2, 128], BF16)
        w2B = Buf("wsl2")
        wbd = A.alloc([128, 16, 128], BF16)
        wbdB = Buf("wbd")
        P.dma("pool", [(wbd[:, :, :], wbd_d[e_])], writes=[wbdB])
        xb = A.alloc([128, S], F32)
        xc = A.alloc([128, S], F32)
        xcb = A.alloc([128, S], BF16)
        gb = A.alloc([128, S], BF16)
        T2 = A.alloc([128, S], F32)
        T3 = A.alloc([128, S], F32)
        T4 = A.alloc([128, S], F32)
        Bxb, Bxc, Bxcb, Bgb, BT2, BT3, BT4 = (Buf(n) for n in ["xb", "xc", "xcb", "gb", "T2", "T3", "T4"])

        def rev(ap2d):
            (ps_, pn_), (fs_, fn_) = ap2d.ap
            return bass.AP(ap2d.tensor, ap2d.offset + (fn_ - 1) * fs_, [[ps_, pn_], [-fs_, fn_]])

        for c in range(4):
            load_w(wsl2, w2B, w_in, [2048 + c * 128, 2560 + c * 128], 128)

            def ev_xb(tc, bk, bkB):
                P.op("dve", lambda e: e.tensor_copy(out=xb[:, tcs(tc)], in_=bk[:, :]), reads=[bkB], writes=[Bxb])

            def ev_gb(tc, bk, bkB):
                P.op("act", lambda e: e.activation(out=gb[:, tcs(tc)], in_=bk[:, :], func=AF.Silu), reads=[bkB], writes=[Bgb])

            proj_feat(wsl2, w2B, 0, ev_xb)
            proj_feat(wsl2, w2B, 1, ev_gb)
            w0, w1, w2, w3 = (cw[:, e_, c, j:j + 1] for j in range(4))
            cbc = cb[:, e_, c:c + 1]
            P.op("dve", lambda e, w2=w2, cbc=cbc: e.tensor_scalar(out=xc[:, :], in0=xb[:, :], scalar1=w2, scalar2=cbc,
                                                                  op0=ALU.mult, op1=ALU.add), reads=[Bxb, B_par], writes=[Bxc])
            P.op("dve", lambda e, w0=w0: e.scalar_tensor_tensor(out=xc[:, 2:S], in0=xb[:, 0:S - 2], scalar=w0, in1=xc[:, 2:S],
                                                                op0=ALU.mult, op1=ALU.add), reads=[Bxb, B_par, Bxc], writes=[Bxc])
            P.op("dve", lambda e, w1=w1: e.scalar_tensor_tensor(out=xc[:, 1:S], in0=xb[:, 0:S - 1], scalar=w1, in1=xc[:, 1:S],
                                                                op0=ALU.mult, op1=ALU.add), reads=[Bxb, B_par, Bxc], writes=[Bxc])
            P.op("dve", lambda e, w3=w3: e.scalar_tensor_tensor(out=xc[:, 0:S - 1], in0=xb[:, 1:S], scalar=w3, in1=xc[:, 0:S - 1],
                                                                op0=ALU.mult, op1=ALU.add), reads=[Bxb, B_par, Bxc], writes=[Bxc])
            P.op("act", lambda e: e.activation(out=xcb[:, :], in_=xc[:, :], func=AF.Copy), reads=[Bxc], writes=[Bxcb])
            for r in range(2):
                Aa, BA = xb, Bxb
                M_, BM = T2, BT2
                X_, BX = (T3, BT3) if r == 0 else (T4, BT4)
                for tc in range(4):
                    for g_, dst, dB, brow in ((0, Aa, BA, 0), (1, X_, BX, 1)):
                        bk, bkB = gen_bank()
                        P.op("pe", lambda e, bk=bk, g_=g_, r=r, c=c, tc=tc: e.matmul(
                            bk[:, :], lhsT=wbd[:, (g_ * 2 + r) * 4 + c, :], rhs=xcb[:, tcs(tc)], start=True, stop=True),
                            reads=[wbdB, Bxcb], writes=[bkB])
                        P.op("act", lambda e, bk=bk, dst=dst, brow=brow, r=r, c=c, tc=tc: e.activation(
                            out=dst[:, tcs(tc)], in_=bk[:, :], func=AF.Sigmoid, bias=lb[:, e_, brow, r, c:c + 1], scale=1.0),
                            reads=[bkB, B_par], writes=[dB])
                P.op("act", lambda e, r=r, c=c: e.activation(out=Aa[:, :], in_=Aa[:, :], func=AF.Exp, scale=nsp8[:, e_, r, c:c + 1]),
                     reads=[BA, B_par], writes=[BA])
                P.op("dve", lambda e: e.tensor_tensor(out=M_[:, :], in0=Aa[:, :], in1=Aa[:, :], op=ALU.mult), reads=[BA], writes=[BM])
                P.op("dve", lambda e: e.tensor_scalar(out=M_[:, :], in0=M_[:, :], scalar1=-1.0, scalar2=1.0, op0=ALU.mult, op1=ALU.add),
                     reads=[BM], writes=[BM])
                P.op("act", lambda e: e.activation(out=M_[:, :], in_=M_[:, :], func=AF.Sqrt), reads=[BM], writes=[BM])
                P.op("dve", lambda e, X_=X_: e.tensor_tensor(out=M_[:, :], in0=M_[:, :], in1=X_[:, :], op=ALU.mult), reads=[BM, BX], writes=[BM])
                P.op("dve", lambda e: e.tensor_tensor(out=M_[:, :], in0=M_[:, :], in1=xc[:, :], op=ALU.mult), reads=[BM, Bxc], writes=[BM])
                if r == 0:
                    P.op("dve", lambda e, X_=X_: e.tensor_tensor_scan(out=X_[:, :], data0=Aa[:, :], data1=M_[:, :], initial=0.0,
                                                                      op0=ALU.mult, op1=ALU.add), reads=[BA, BM], writes=[BX])
                else:
                    P.op("dve", lambda e, X_=X_: e.tensor_tensor_scan(out=rev(X_[:, :]), data0=rev(Aa[:, :]), data1=rev(M_[:, :]), initial=0.0,
                                                                      op0=ALU.mult, op1=ALU.add), reads=[BA, BM], writes=[BX])
            P.op("dve", lambda e: e.tensor_tensor(out=T3[:, :], in0=T3[:, :], in1=T4[:, :], op=ALU.add), reads=[BT3, BT4], writes=[BT3])
            P.op("dve", lambda e, c=c: e.tensor_tensor(out=mixT[:, 4 + c, :], in0=T3[:, :], in1=gb[:, :], op=ALU.mult),
                 reads=[BT3, Bgb], writes=[B_mix[4 + c]])
        out_proj(l, b, ewout_d[e_])

    def odd_layer(l, b):
        o_ = l // 2
        w_in = owin_d[o_]
        phase()
        make_hT(l, b)
        phase()
        wsl = [A.alloc([128, 8, 4, 128], BF16) for _ in range(2)]
        wB = [Buf("wsl0"), Buf("wsl1")]
        qT2 = A.alloc([128, 2, S], BF16)
        kT = A.alloc([128, S], BF16)
        gT = A.alloc([128, S], BF16)
        v = A.alloc([128, 16, 2, 65], BF16)
        nb32 = A.alloc([128, 2, NT, 128], F32)
        enb = A.alloc([128, 2, NT, 128], BF16)
        Bq, Bk, Bg, Bv, Bnb = Buf("q"), Buf("k"), Buf("g"), Buf("v"), Buf("nb")
        NE = 3
        Es = [A.alloc([128, 512], BF16) for _ in range(NE)]
        EB = [Buf(f"E{i}") for i in range(NE)]
        P.op("dve", lambda e: e.memset(qT2[64:128, 0, :], 0.0), writes=[Bq])
        P.op("dve", lambda e: e.memset(qT2[0:64, 1, :], 0.0), writes=[Bq])
        obf = [A.alloc([128, 128], BF16) for _ in range(2)]
        Bo = [Buf("o0"), Buf("o1")]
        small = A.alloc([128, 4], F32)
        Bsm = Buf("small")
        for hp in range(8):
            sl = hp % 2
            load_w(wsl[sl], wB[sl], w_in, [hp * 128, 1024 + hp * 128, 2048 + hp * 128, 3072 + hp * 128], 128)
            P.dma("sp", [(nb32[:, :, :, :], nab_d[o_, hp])], writes=[Bnb])
            P.op("act", lambda e: e.activation(out=enb[:, :, :, :], in_=nb32[:, :, :, :], func=AF.Exp), reads=[Bnb], writes=[Bnb])
            P.op("dve", lambda e: e.memset(v[:, :, :, 64:65], 1.0), writes=[Bv])
            pend = []

            def ev_q(tc, bk, bkB):
                for m_ in range(2):
                    P.op("dve", lambda e, m_=m_: e.tensor_scalar(out=qT2[m_ * 64:(m_ + 1) * 64, m_, tcs(tc)], in0=bk[m_ * 64:(m_ + 1) * 64, :],
                                                                 scalar1=0.125, scalar2=None, op0=ALU.mult),
                         reads=[bkB], writes=[Bq])

            def ev_k(tc, bk, bkB):
                P.op("dve", lambda e: e.tensor_copy(out=kT[:, tcs(tc)], in_=bk[:, :]), reads=[bkB], writes=[Bk])

            def ev_g(tc, bk, bkB):
                P.op("act", lambda e: e.activation(out=gT[:, tcs(tc)], in_=bk[:, :], func=AF.Silu), reads=[bkB], writes=[Bg])

            proj_feat(wsl[sl], wB[sl], 0, ev_q)
            proj_feat(wsl[sl], wB[sl], 1, ev_k)
            proj_feat(wsl[sl], wB[sl], 3, ev_g)
            for g4 in range(4):
                bk, bkB = gen_bank()
                for i in range(4):
                    tt = g4 * 4 + i
                    for c in range(8):
                        P.op("pe", lambda e, c=c, bk=bk, i=i, tt=tt, sl=sl: e.matmul(
                            bk[:, i * 128:(i + 1) * 128], lhsT=hT[:, c, tt * 128:(tt + 1) * 128], rhs=wsl[sl][:, c, 2, :],
                            start=(c == 0), stop=(c == 7)), reads=[wB[sl], B_hT[tt // 4]], writes=[bkB])
                P.op("act", lambda e, bk=bk, g4=g4: e.activation(
                    out=v[:, g4 * 4:(g4 + 1) * 4, :, 0:64], in_=bk[:, :].rearrange("p (a b c) -> p a b c", a=4, b=2), func=AF.Copy),
                    reads=[bkB], writes=[Bv])
            work = []
            for rp in range(16):
                ch = NA_PLAN[rp]
                groups = [ch[i:i + 2] for i in range(0, len(ch), 2)]
                for gi, grp in enumerate(groups):
                    work.append((rp, gi, len(groups), grp))
            state = {"n": 0}

            def qk(w):
                rp, gi, ng_, grp = w
                si = state["n"] % 2
                ei = state["n"] % NE
                state["n"] += 1
                Sb, SB = banks[2 + si], bankB[2 + si]
                for ci, (j, t) in enumerate(grp):
                    for hh in range(2):
                        col = (ci * 2 + hh) * 128
                        P.op("pe", lambda e, Sb=Sb, col=col, hh=hh, j=j, rp=rp: e.matmul(
                            Sb[:, col:col + 128], lhsT=kT[:, j * 128:(j + 1) * 128],
                            rhs=qT2[:, hh, rp * 128:(rp + 1) * 128], start=True, stop=True),
                            reads=[Bq, Bk], writes=[SB])
                ncol = len(grp) * 256
                P.op("act", lambda e, Sb=Sb, ei=ei, ncol=ncol: e.activation(out=Es[ei][:, 0:ncol], in_=Sb[:, 0:ncol], func=AF.Exp),
                     reads=[SB], writes=[EB[ei]])
                for ci, (j, t) in enumerate(grp):
                    P.op("dve", lambda e, ei=ei, ci=ci, t=t: e.tensor_tensor(
                        out=Es[ei][:, ci * 256:(ci + 1) * 256].rearrange("p (h q) -> p h q", h=2),
                        in0=Es[ei][:, ci * 256:(ci + 1) * 256].rearrange("p (h q) -> p h q", h=2),
                        in1=enb[:, :, t, :], op=ALU.mult), reads=[Bnb], writes=[EB[ei]])
                return ei

            def av(w, si):
                rp, gi, ng_, grp = w
                Ob, OBf = banks[4 + rp % 4], bankB[4 + rp % 4]
                for ci, (j, t) in enumerate(grp):
                    for hh in range(2):
                        col = (ci * 2 + hh) * 128
                        first = (gi == 0 and ci == 0)
                        last = (gi == ng_ - 1 and ci == len(grp) - 1)
                        P.op("pe", lambda e, Ob=Ob, col=col, hh=hh, j=j, si=si, first=first, last=last: e.matmul(
                            Ob[:, hh * 65:(hh + 1) * 65], lhsT=Es[si][:, col:col + 128], rhs=v[:, j, hh, :],
                            start=(first and hh == 0), stop=last, skip_group_check=True),
                            reads=[EB[si], Bv], writes=[OBf])
                for f_ in pend:
                    f_()
                pend.clear()
                if gi == ng_ - 1:
                    post(rp, Ob, OBf)

            def post(rp, Ob, OBf, hp=hp):
                pi = rp % 2
                O3 = Ob[:, 0:130].rearrange("p (a b) -> p a b", a=2)
                P.op("dve", lambda e: e.reciprocal(out=small[:, 0:2], in_=O3[:, :, 64]), reads=[OBf], writes=[Bsm])
                for hh in range(2):
                    P.op("dve", lambda e, hh=hh: e.tensor_scalar(out=obf[pi][:, hh * 64:(hh + 1) * 64], in0=O3[:, hh, 0:64],
                                                                 scalar1=small[:, hh:hh + 1], scalar2=None, op0=ALU.mult),
                         reads=[OBf, Bsm], writes=[Bo[pi]])

                def post_b():
                    bk, bkB = gen_bank()
                    bkb = bk[:, :].bitcast(BF16)
                    P.op("pe", lambda e: e.transpose(out=bkb[:, 0:128], in_=obf[pi], identity=identb[:]), reads=[Bo[pi], B_const], writes=[bkB])
                    P.op("dve", lambda e: e.tensor_tensor(out=mixT[:, hp, rp * 128:(rp + 1) * 128], in0=bkb[:, 0:128],
                                                          in1=gT[:, rp * 128:(rp + 1) * 128], op=ALU.mult),
                         reads=[bkB, Bg], writes=[B_mix[hp]])
                pend.append(post_b)

            si_cur = qk(work[0])
            for wi in range(len(work)):
                si_next = None
                if wi + 1 < len(work):
                    si_next = qk(work[wi + 1])
                av(work[wi], si_cur)
                si_cur = si_next
            for f_ in pend:
                f_()
            pend.clear()
        out_proj(l, b, owout_d[o_])

    final_toks = []
    for b in range(NB):
        phase()
        xtok = [A.alloc([128, D], F32) for _ in range(3)]
        xB = [Buf(f"xtok{i}") for i in range(3)]
        for tt in range(16):
            sl = tt % 3
            P.dma("sp", [(xtok[sl][:, :], x_d[b, tt * 128:(tt + 1) * 128, :])], writes=[xB[sl]])
            for half in range(2):
                bk, bkB = gen_bank()
                for i in range(4):
                    c = half * 4 + i
                    P.op("pe", lambda e, bk=bk, i=i, c=c, sl=sl: e.transpose(out=bk[:, i * 128:(i + 1) * 128],
                                                                             in_=xtok[sl][:, c * 128:(c + 1) * 128], identity=ident32[:]),
                         reads=[xB[sl], B_const], writes=[bkB])
                eng = "dve" if half == 0 else "act"
                if eng == "dve":
                    P.op("dve", lambda e, bk=bk, half=half, tt=tt: e.tensor_copy(
                        out=xT[:, half * 4:(half + 1) * 4, tt * 128:(tt + 1) * 128], in_=bk[:, :].rearrange("p (a b) -> p a b", a=4)),
                        reads=[bkB], writes=[B_xT[tt // 4]])
                else:
                    P.op("act", lambda e, bk=bk, half=half, tt=tt: e.activation(
                        out=xT[:, half * 4:(half + 1) * 4, tt * 128:(tt + 1) * 128], in_=bk[:, :].rearrange("p (a b) -> p a b", a=4), func=AF.Copy),
                        reads=[bkB], writes=[B_xT[tt // 4]])
        for l in range(nlayers):
            if l % 2 == 0:
                even_layer(l, b)
            else:
                odd_layer(l, b)
        phase()
        sq_slots = [A.alloc([128, 512], BF16) for _ in range(4)]
        sqB = [Buf(f"sq{i}") for i in range(4)]
        lnv = A.alloc([128, 512], F32)
        rstd = A.alloc([128, 512], F32)
        rB = Buf("rstd")
        yT = A.alloc([128, 8, 512], F32)
        yB = Buf("yT")
        otok = [A.alloc([128, D], F32) for _ in range(2)]
        oB = [Buf("otok0"), Buf("otok1")]
        k = 0
        for tc in range(4):
            rms_stats(tc, (sq_slots, sqB, lnv, rstd, rB), [B_xT[tc]])
            for c in range(8):
                P.op("dve", lambda e, c=c, tc=tc: e.scalar_tensor_tensor(
                    out=yT[:, c, :], in0=xT[:, c, tcs(tc)], scalar=fg[:, c:c + 1], in1=rstd, op0=ALU.mult, op1=ALU.mult),
                    reads=[B_xT[tc], rB, B_par], writes=[yB])
            for i4 in range(4):
                tt = tc * 4 + i4
                sl = k % 2
                k += 1
                for half in range(2):
                    bk, bkB = gen_bank()
                    for i in range(4):
                        c = half * 4 + i
                        P.op("pe", lambda e, bk=bk, i=i, c=c, i4=i4: e.transpose(
                            out=bk[:, i * 128:(i + 1) * 128], in_=yT[:, c, i4 * 128:(i4 + 1) * 128], identity=ident32[:]),
                            reads=[yB, B_const], writes=[bkB])
                    if half == 0:
                        P.op("dve", lambda e, bk=bk, sl=sl, half=half: e.tensor_copy(out=otok[sl][:, half * 512:(half + 1) * 512], in_=bk[:, :]),
                             reads=[bkB], writes=[oB[sl]])
                    else:
                        P.op("act", lambda e, bk=bk, sl=sl, half=half: e.activation(out=otok[sl][:, half * 512:(half + 1) * 512], in_=bk[:, :], func=AF.Copy),
                             reads=[bkB], writes=[oB[sl]])
                final_toks.append(P.dma("sp", [(out_d[b, tt * 128:(tt + 1) * 128, :], otok[sl][:, :])], reads=[oB[sl]]))
    P.emit(final_toks)
    return nc


_NC_CACHE = {}


def _prep_shared(inp):
    f = lambda a: np.ascontiguousarray(np.asarray(a, dtype=np.float32))
    sh = {}
    sh["ada_w"] = f(inp["ada_w"])
    sh["ada_b_l"] = f(np.transpose(np.asarray(inp["ada_b"]).reshape(4, 24, 128), (2, 0, 1)))
    sh["norm_g_l"] = f(np.transpose(np.asarray(inp["norm_g"]).reshape(4, 8, 128), (2, 0, 1)))
    sh["final_g_l"] = f(np.asarray(inp["final_g"]).reshape(8, 128).T)
    sh["even_w_in"] = f(inp["even_w_in"])
    sh["even_w_out"] = f(inp["even_w_out"])
    sh["odd_w_in"] = f(inp["odd_w_in"])
    sh["odd_w_out"] = f(inp["odd_w_out"])
    sh["t5bt"] = _t5_bias_host(np.asarray(inp["t5_table"], dtype=np.float32))
    sh["da_lam_f"] = f(np.asarray(inp["da_lam"]).reshape(1, 512))
    sh["subln_g_l"] = f(np.asarray(inp["da_subln_g"]).T)
    sh["conv_w_l"] = f(np.transpose(np.asarray(inp["lru_conv_w"]).reshape(2, 4, 4, 128), (3, 0, 2, 1)))
    sh["conv_b_l"] = f(np.transpose(np.asarray(inp["lru_conv_b"]).reshape(2, 4, 128), (2, 0, 1)))
    wa = np.asarray(inp["lru_w_a"], dtype=np.float32)
    wx = np.asarray(inp["lru_w_x"], dtype=np.float32)
    wbd = np.zeros((2, 128, 2, 2, 4, 128), dtype=np.float32)
    for g_, w in enumerate((wa, wx)):
        for c in range(4):
            for half in range(2):
                blk = w[:, :, 2 * c + half]
                wbd[:, half * 64:(half + 1) * 64, g_, :, c, half * 64:(half + 1) * 64] = np.transpose(blk, (0, 2, 1, 3))
    sh["lru_wbd"] = np.ascontiguousarray(wbd.reshape(2, 128, 16, 128))
    lbias = np.stack([np.asarray(inp["lru_b_a"]), np.asarray(inp["lru_b_x"]), np.asarray(inp["lru_lambda"])], axis=1)
    sh["lru_bias_l"] = f(np.transpose(lbias.reshape(2, 3, 2, 4, 128), (4, 0, 1, 2, 3)))
    sh["na_bias"] = _na_bias_host(np.asarray(inp["na_rpb"], dtype=np.float32))
    return sh


def kernel(nlayers=4, cores=8, **inp):
    x = np.asarray(inp["x"], dtype=np.float32)
    c = np.asarray(inp["c"], dtype=np.float32)
    sh = _prep_shared(inp)
    key = nlayers
    if key not in _NC_CACHE:
        _NC_CACHE[key] = build_program(nlayers)
    nc = _NC_CACHE[key]
    in_maps = []
    for i in range(cores):
        m = dict(sh)
        m["x"] = np.ascontiguousarray(x[NB * i:NB * (i + 1)])
        cc = c[NB * i:NB * (i + 1)]
        m["c_l"] = np.ascontiguousarray(np.transpose(cc.T.reshape(8, 128, NB), (1, 0, 2)))
        in_maps.append(m)
    res = run_bass_kernel_spmd(nc, in_maps, core_ids=list(range(cores)))
    out = np.concatenate([np.asarray(r["out"], dtype=np.float32) for r in res.results], axis=0)
    return out
```

```python
import math
import numpy as np
import concourse.bass as bass
import concourse.mybir as mybir
from concourse.bass_utils import run_bass_kernel_spmd

F32 = mybir.dt.float32
BF16 = mybir.dt.bfloat16
ALU = mybir.AluOpType
AF = mybir.ActivationFunctionType
AX = mybir.AxisListType

S = 2048
D = 1024
NB = 2
EPS = 1e-6
NEG = -30000.0
LAM_INIT = {0: 0.8 - 0.6 * math.exp(-0.3 * 0), 2: 0.8 - 0.6 * math.exp(-0.3 * 2)}


class Buf:
    __slots__ = ("name", "w", "r")

    def __init__(self, name):
        self.name = name
        self.w = None
        self.r = []


class Prog:
    ENGS = ["pe", "act", "dve", "pool", "sp"]
    SEM_LIMIT = 16000

    def __init__(self, nc):
        self.nc = nc
        self.ops = {e: [] for e in self.ENGS}
        self.cur = {}
        self.nsem = 0
        for e in self.ENGS:
            self._new_sem(e)
        self.dma_sems = {}
        self.dma_rr = {}
        self.pending = {e: [] for e in self.ENGS}
        self.dma_out = []

    def _new_sem(self, e):
        self.nsem += 1
        self.cur[e] = [self.nc.alloc_semaphore(name=f"s_{e}_{self.nsem}"), 0, e]

    def _tok(self, e):
        c = self.cur[e]
        if c[1] >= self.SEM_LIMIT:
            self._new_sem(e)
            c = self.cur[e]
        c[1] += 1
        return (c[0], c[1], e)

    @staticmethod
    def _deps(reads, writes, extra):
        waits = list(extra)
        for b in reads:
            if b.w is not None:
                waits.append(b.w)
        for b in writes:
            if b.w is not None:
                waits.append(b.w)
            waits.extend(b.r)
        return waits

    def op(self, eng, fn, reads=(), writes=(), extra=()):
        waits = self._deps(reads, writes, extra) + self.pending[eng]
        self.pending[eng] = []
        if eng == "pe":
            waits = [w for w in waits if w[2] != "pe"]
        tok = self._tok(eng)
        self.ops[eng].append((waits, fn, tok, 1))
        for b in reads:
            self._addr(b, tok)
        for b in writes:
            b.w = tok
            b.r = []
        return tok

    @staticmethod
    def _addr(b, tok):
        for i, t in enumerate(b.r):
            if t[0] is tok[0]:
                if tok[1] > t[1]:
                    b.r[i] = tok
                return
        b.r.append(tok)

    def dma(self, queue, parts, reads=(), writes=(), extra=()):
        waits = self._deps(reads, writes, extra) + self.pending[queue]
        self.pending[queue] = []
        NS = 8
        lst = self.dma_sems.setdefault(queue, [])
        if len(lst) < NS:
            self.nsem += 1
            s = [self.nc.alloc_semaphore(name=f"d_{queue}_{self.nsem}"), 0, None]
            lst.append(s)
        else:
            i = self.dma_rr.get(queue, 0)
            s = lst[i % NS]
            self.dma_rr[queue] = i + 1
            if s[1] > 30000:
                self.nsem += 1
                s2 = [self.nc.alloc_semaphore(name=f"d_{queue}_{self.nsem}"), 0, s[2]]
                lst[i % NS] = s2
                s = s2
        if s[2] is not None:
            waits.append(s[2])
        s[1] += 16 * len(parts)
        tok = (s[0], s[1], "dma")
        s[2] = tok

        def fn(eng, parts=parts, sem=s[0]):
            for (o, i_) in parts:
                eng.dma_start(out=o, in_=i_).then_inc(sem, 16)
            return None

        self.ops[queue].append((waits, fn, None, 0))
        for b in reads:
            self._addr(b, tok)
        for b in writes:
            b.w = tok
            b.r = []
        self.dma_out.append(tok)
        return tok

    def barrier(self):
        toks = []
        for e in self.ENGS:
            c = self.cur[e]
            if c[1] > 0:
                toks.append((c[0], c[1], e))
        toks.extend(self.dma_out)
        self.dma_out = []
        for e in self.ENGS:
            self.pending[e] = list(toks)

    def emit(self, final_waits):
        nc = self.nc
        engmap = {"pe": "tensor", "act": "scalar", "dve": "vector", "pool": "gpsimd", "sp": "sync"}
        with nc.Block() as block:
            for e in self.ENGS:
                ops = self.ops[e]

                def body(eng, ops=ops, e=e):
                    seen = {}
                    for (waits, fn, tok, inc) in ops:
                        for w in waits:
                            k = id(w[0])
                            if seen.get(k, 0) >= w[1]:
                                continue
                            eng.wait_ge(w[0], w[1])
                            seen[k] = w[1]
                        ins = fn(eng)
                        if inc:
                            ins.then_inc(tok[0], 1)
                    if e == "sp":
                        for w in final_waits:
                            eng.wait_ge(w[0], w[1])

                getattr(block, engmap[e])(body)


class Arena:
    def __init__(self, tensor, nwords):
        self.t = tensor
        self.n = nwords
        self.off = 0

    def reset(self):
        self.off = 0

    def alloc(self, shape, dt):
        per = int(np.prod(shape[1:]))
        words = per if dt == F32 else (per + 1) // 2
        words = (words + 15) // 16 * 16
        assert self.off + words <= self.n, f"arena overflow {self.off}+{words}>{self.n}"
        v = self.t[:, self.off:self.off + words]
        self.off += words
        if dt != F32:
            v = v.bitcast(dt)
        v = v[:, 0:per]
        if len(shape) == 3:
            v = v.rearrange("p (a b) -> p a b", a=shape[1])
        elif len(shape) == 4:
            v = v.rearrange("p (a b c) -> p a b c", a=shape[1], b=shape[2])
        return v


def _t5_bucket_np(rel):
    nb = 16
    max_exact = 8
    ret = np.where(rel > 0, nb, 0)
    n = np.abs(rel)
    large = np.full(n.shape, 15, dtype=np.int64)
    bounds = [8, 12, 16, 23, 32, 46, 64, 91, 128]
    for j in range(8):
        large = np.where((n >= bounds[j]) & (n < bounds[j + 1]), 8 + j, large)
    large = np.minimum(large, nb - 1)
    return ret + np.where(n < max_exact, n, large)


def _na_plan():
    rows = 32
    tiles = []
    plan = []
    for rp in range(16):
        r0 = 2 * rp
        win = []
        for b in range(2):
            r = r0 + b
            rs = min(max(r - 4, 0), rows - 8)
            win.append((rs, rs + 7))
        lo = min(w[0] for w in win) // 2
        hi = max(w[1] for w in win) // 2
        lst = []
        for j in range(lo, hi + 1):
            valid = tuple(tuple(int(win[b][0] <= 2 * j + a <= win[b][1]) for b in range(2)) for a in range(2))
            if not any(any(v) for v in valid):
                continue
            key = (2 * j - r0, valid)
            if key not in tiles:
                tiles.append(key)
            lst.append((j, tiles.index(key)))
        plan.append(lst)
    return plan, tiles


def _sorted_plan():
    plan, tiles = _na_plan()
    order = sorted(range(len(tiles)), key=lambda i: (tiles[i][0], -sum(sum(v) for v in tiles[i][1])))
    remap = {old: new for new, old in enumerate(order)}
    return [[(j, remap[t]) for (j, t) in lst] for lst in plan], [tiles[i] for i in order]


NA_PLAN, NA_TILES = _sorted_plan()
NT = len(NA_TILES)


def _na_bias_host(rpb):
    cols = np.arange(64)
    cs = np.clip(cols - 8, 0, 48)
    colmask = (cols[None, :] >= cs[:, None]) & (cols[None, :] < cs[:, None] + 16)
    dc = cols[None, :] - cols[:, None] + 15
    dc_cl = np.clip(dc, 0, 30)
    out = np.full((2, 16, NT, 128, 128), NEG, dtype=np.float32)
    for t, (Dd, valid) in enumerate(NA_TILES):
        for a in range(2):
            for b in range(2):
                if not valid[a][b]:
                    continue
                dr = Dd + a - b + 7
                assert 0 <= dr <= 14
                g = rpb[:, :, dr, :][:, :, dc_cl]
                g = np.where(colmask[None, None], g, NEG)
                out[:, :, t, a * 64:(a + 1) * 64, b * 64:(b + 1) * 64] = np.transpose(g, (0, 1, 3, 2))
    out = out.reshape(2, 8, 2, NT, 128, 128)
    out = np.transpose(out, (0, 1, 4, 3, 2, 5))
    return np.ascontiguousarray(out)


BTW = 896


def _t5_bias_host(t5_table):
    p = np.arange(128)[:, None]
    j = np.arange(BTW)[None, :]
    rel = p - j + 384
    bk = _t5_bucket_np(rel)
    return np.ascontiguousarray(t5_table[:, bk].astype(np.float32))


def build_program(nlayers=4):
    nc = bass.Bass("TRN2", target_bir_lowering=False)

    def din(name, shape):
        return nc.dram_tensor(name, list(shape), F32, kind="ExternalInput").ap()

    x_d = din("x", [NB, S, D])
    c_d = din("c_l", [128, 8, NB])
    adaw_d = din("ada_w", [4, D, 3 * D])
    adab_d = din("ada_b_l", [128, 4, 24])
    ng_d = din("norm_g_l", [128, 4, 8])
    fg_d = din("final_g_l", [128, 8])
    ewin_d = din("even_w_in", [2, D, 3072])
    ewout_d = din("even_w_out", [2, D, D])
    owin_d = din("odd_w_in", [2, D, 4096])
    owout_d = din("odd_w_out", [2, D, D])
    t5_d = din("t5bt", [4, 128, BTW])
    lam_d = din("da_lam_f", [1, 512])
    sg_d = din("subln_g_l", [128, 2])
    cw_d = din("conv_w_l", [128, 2, 4, 4])
    cb_d = din("conv_b_l", [128, 2, 4])
    wbd_d = din("lru_wbd", [2, 128, 16, 128])
    lb_d = din("lru_bias_l", [128, 2, 3, 2, 4])
    nab_d = din("na_bias", [2, 8, 128, NT, 2, 128])
    out_d = nc.dram_tensor("out", [NB, S, D], F32, kind="ExternalOutput").ap()

    P = Prog(nc)

    xT = nc.alloc_sbuf_tensor("xT", [128, 8, S], F32)
    hT = nc.alloc_sbuf_tensor("hT", [128, 8, S], BF16)
    mixT = nc.alloc_sbuf_tensor("mixT", [128, 8, S], BF16)
    ident32 = nc.alloc_sbuf_tensor("ident32", [128, 128], F32)
    identb = nc.alloc_sbuf_tensor("identb", [128, 128], BF16)
    ones32 = nc.alloc_sbuf_tensor("ones32", [128, 128], F32)
    onesb = nc.alloc_sbuf_tensor("onesb", [128, 128], BF16)
    cact = nc.alloc_sbuf_tensor("cact", [128, 8, NB], F32)
    modT = nc.alloc_sbuf_tensor("modT", [128, 4, 24, NB], F32)
    adab = nc.alloc_sbuf_tensor("adab", [128, 4, 24], F32)
    ng = nc.alloc_sbuf_tensor("ng", [128, 4, 8], F32)
    fg = nc.alloc_sbuf_tensor("fg", [128, 8], F32)
    scm = nc.alloc_sbuf_tensor("scm", [128, 4, NB, 8], F32)
    lamb = nc.alloc_sbuf_tensor("lamb", [128, 512], F32)
    lamt = nc.alloc_sbuf_tensor("lamt", [128, 128], F32)
    lamc = nc.alloc_sbuf_tensor("lamc", [128, 16], F32)
    sgl = nc.alloc_sbuf_tensor("sgl", [128, 2], F32)
    cw = nc.alloc_sbuf_tensor("cw", [128, 2, 4, 4], F32)
    cb = nc.alloc_sbuf_tensor("cb", [128, 2, 4], F32)
    lb = nc.alloc_sbuf_tensor("lb", [128, 2, 3, 2, 4], F32)
    nsp8 = nc.alloc_sbuf_tensor("nsp8", [128, 2, 2, 4], F32)
    rem_words = (nc.sbuf_bytes_remaining - 256) // 4
    rem_words = rem_words // 16 * 16
    arena_t = nc.alloc_sbuf_tensor("arena", [128, rem_words], F32)
    A = Arena(arena_t, rem_words)
    banks = [nc.alloc_psum_tensor(f"bank{i}", [128, 512], F32) for i in range(8)]
    bankB = [Buf(f"bank{i}") for i in range(8)]

    B_xT = [Buf(f"xT{i}") for i in range(4)]
    B_hT = [Buf(f"hT{i}") for i in range(4)]
    B_mix = [Buf(f"mix{i}") for i in range(8)]
    B_const = Buf("const")
    B_par = Buf("params")

    gen_rr = [0]

    def gen_bank():
        i = gen_rr[0] % 2
        gen_rr[0] += 1
        return banks[i], bankB[i]

    def tcs(tc):
        return slice(tc * 512, (tc + 1) * 512)

    def phase():
        P.barrier()
        A.reset()

    eps_col = lamc[:, 12:13]

    P.op("pool", lambda e: e.memset(ident32[:], 1.0), writes=[B_const])
    P.op("pool", lambda e: e.affine_select(out=ident32[:], in_=ident32[:], pattern=[[-1, 128]], compare_op=ALU.is_equal,
                                           fill=0.0, base=0, channel_multiplier=1), reads=[B_const], writes=[B_const])
    P.op("dve", lambda e: e.tensor_copy(out=identb[:], in_=ident32[:]), reads=[B_const], writes=[B_const])
    P.op("dve", lambda e: e.memset(ones32[:], 1.0), writes=[B_const])
    P.op("dve", lambda e: e.memset(onesb[:], 1.0), writes=[B_const])
    P.op("dve", lambda e: e.memset(lamc[:], 0.0), writes=[B_par])
    P.op("dve", lambda e: e.memset(eps_col, EPS), writes=[B_par])
    P.dma("sp", [(cact[:], c_d), (adab[:], adab_d), (ng[:], ng_d), (fg[:], fg_d), (sgl[:], sg_d), (cw[:], cw_d),
                 (cb[:], cb_d), (lb[:], lb_d), (lamb[:], lam_d.partition_broadcast(128))], writes=[B_par])
    P.op("act", lambda e: e.activation(out=cact[:], in_=cact[:], func=AF.Silu), reads=[B_par], writes=[B_par])

    wp_slots = [A.alloc([128, 8, 512], F32) for _ in range(2)]
    wp_B = [Buf("wp0"), Buf("wp1")]
    B_mod = Buf("mod")
    it = 0
    for l in range(nlayers):
        src = adaw_d[l].rearrange("(c p) n -> p c n", p=128)
        for piece in range(6):
            sl = it % 2
            it += 1
            wpt, wpb = wp_slots[sl], wp_B[sl]
            P.dma("sp", [(wpt[:, 0:4, :], src[:, 0:4, piece * 512:(piece + 1) * 512]),
                         (wpt[:, 4:8, :], src[:, 4:8, piece * 512:(piece + 1) * 512])], writes=[wpb])
            bk, bkB = gen_bank()
            for fc in range(4):
                for kc in range(8):
                    P.op("pe", lambda e, bk=bk, wpt=wpt, fc=fc, kc=kc: e.matmul(
                        bk[:, fc * NB:(fc + 1) * NB], lhsT=wpt[:, kc, fc * 128:(fc + 1) * 128], rhs=cact[:, kc, :],
                        start=(kc == 0), stop=(kc == 7)), reads=[wpb, B_par], writes=[bkB])
            P.op("dve", lambda e, bk=bk, l=l, piece=piece: e.tensor_tensor(
                out=modT[:, l, piece * 4:(piece + 1) * 4, :],
                in0=bk[:, 0:4 * NB].rearrange("p (a b) -> p a b", a=4),
                in1=adab[:, l, piece * 4:(piece + 1) * 4].unsqueeze(2).to_broadcast([128, 4, NB]),
                op=ALU.add), reads=[bkB, B_par], writes=[B_mod])
    for l in range(nlayers):
        for b in range(NB):
            P.op("dve", lambda e, l=l, b=b: e.scalar_tensor_tensor(
                out=scm[:, l, b, :], in0=modT[:, l, 8:16, b], scalar=1.0, in1=ng[:, l, :], op0=ALU.add, op1=ALU.mult),
                reads=[B_mod, B_par], writes=[B_mod])
    for e_ in range(2):
        for i in range(2):
            base = e_ * 256 + i * 128
            P.op("dve", lambda e, base=base: e.tensor_tensor(out=lamt[:, 0:64], in0=lamb[:, base:base + 64],
                                                             in1=lamb[:, base + 64:base + 128], op=ALU.mult),
                 reads=[B_par], writes=[B_par])
            P.op("dve", lambda e, e_=e_, i=i: e.reduce_sum(out=lamc[:, e_ * 2 + i:e_ * 2 + i + 1], in_=lamt[:, 0:64], axis=AX.X),
                 reads=[B_par], writes=[B_par])
    P.op("act", lambda e: e.activation(out=lamc[:, 4:8], in_=lamc[:, 0:4], func=AF.Exp), reads=[B_par], writes=[B_par])
    for e_ in range(2):
        li = LAM_INIT[2 * e_]
        P.op("dve", lambda e, e_=e_, li=li: e.scalar_tensor_tensor(
            out=lamc[:, 8 + e_:9 + e_], in0=lamc[:, 5 + 2 * e_:6 + 2 * e_], scalar=-li, in1=lamc[:, 4 + 2 * e_:5 + 2 * e_],
            op0=ALU.add, op1=ALU.subtract), reads=[B_par], writes=[B_par])
        P.op("dve", lambda e, e_=e_, li=li: e.tensor_scalar(
            out=lamc[:, 10 + e_:11 + e_], in0=sgl[:, e_:e_ + 1], scalar1=(1.0 - li), scalar2=None, op0=ALU.mult),
            reads=[B_par], writes=[B_par])
    P.op("act", lambda e: e.activation(out=nsp8[:], in_=lb[:, :, 2, :, :], func=AF.Exp, scale=-1.0), reads=[B_par], writes=[B_par])
    P.op("act", lambda e: e.activation(out=nsp8[:], in_=nsp8[:], func=AF.Ln, bias=1.0), reads=[B_par], writes=[B_par])
    P.op("dve", lambda e: e.tensor_scalar(out=nsp8[:], in0=nsp8[:], scalar1=-8.0, scalar2=None, op0=ALU.mult),
         reads=[B_par], writes=[B_par])
    B_par_all = [B_par, B_mod, B_const]

    def rms_stats(tc, tmp, reads):
        sq_slots, sqB, lnv, rstd, rB = tmp
        bk, bkB = gen_bank()
        for c in range(8):
            sl = c % len(sq_slots)
            P.op("act", lambda e, c=c, sl=sl: e.activation(out=sq_slots[sl], in_=xT[:, c, tcs(tc)], func=AF.Square),
                 reads=reads, writes=[sqB[sl]])
            P.op("pe", lambda e, c=c, sl=sl, bk=bk: e.matmul(bk[:, :], lhsT=onesb[:], rhs=sq_slots[sl], start=(c == 0), stop=(c == 7)),
                 reads=[sqB[sl], B_const], writes=[bkB])
        P.op("act", lambda e, bk=bk: e.activation(out=lnv, in_=bk[:, :], func=AF.Ln, scale=1.0 / D, bias=eps_col),
             reads=[bkB, B_par], writes=[rB])
        P.op("act", lambda e: e.activation(out=rstd, in_=lnv, func=AF.Exp, scale=-0.5), reads=[rB], writes=[rB])
        return rstd, rB

    def make_hT(l, b):
        sq_slots = [A.alloc([128, 512], BF16) for _ in range(4)]
        sqB = [Buf(f"sq{i}") for i in range(4)]
        lnv = A.alloc([128, 512], F32)
        rstd = A.alloc([128, 512], F32)
        rB = Buf("rstd")
        t_slots = [A.alloc([128, 512], F32) for _ in range(2)]
        tB = [Buf("t0"), Buf("t1")]
        k = 0
        for tc in range(4):
            rstd_, rB_ = rms_stats(tc, (sq_slots, sqB, lnv, rstd, rB), [B_xT[tc]])
            for c in range(8):
                sl = k % 2
                k += 1
                P.op("dve", lambda e, c=c, sl=sl, tc=tc: e.scalar_tensor_tensor(
                    out=t_slots[sl], in0=xT[:, c, tcs(tc)], scalar=scm[:, l, b, c:c + 1], in1=rstd, op0=ALU.mult, op1=ALU.mult),
                    reads=[B_xT[tc], rB, B_mod], writes=[tB[sl]])
                P.op("act", lambda e, c=c, sl=sl, tc=tc: e.activation(
                    out=hT[:, c, tcs(tc)], in_=t_slots[sl], func=AF.Identity, bias=modT[:, l, c, b:b + 1], scale=1.0),
                    reads=[tB[sl], B_mod], writes=[B_hT[tc]])

    def load_w(dst, dstB, w2d, colbases, width):
        src = w2d.rearrange("(c p) n -> p c n", p=128)
        parts = [(dst[:, :, s_, :], src[:, :, cb_:cb_ + width]) for s_, cb_ in enumerate(colbases)]
        P.dma("pool", parts, writes=[dstB])

    def proj_feat(wsl, wB, s_, evac):
        for tc in range(4):
            bk, bkB = gen_bank()
            for c in range(8):
                P.op("pe", lambda e, c=c, bk=bk, tc=tc: e.matmul(bk[:, :], lhsT=wsl[:, c, s_, :], rhs=hT[:, c, tcs(tc)],
                                                                 start=(c == 0), stop=(c == 7)),
                     reads=[wB, B_hT[tc]], writes=[bkB])
            evac(tc, bk, bkB)

    def out_proj(l, b, w_out2d):
        phase()
        wout = A.alloc([128, 8, D], BF16)
        woB = Buf("wout")
        src = w_out2d.rearrange("(c p) n -> p c n", p=128)
        P.dma("pool", [(wout[:, 0:4, :], src[:, 0:4, :]), (wout[:, 4:8, :], src[:, 4:8, :])], writes=[woB])
        for tc in range(4):
            for fo in range(8):
                bk, bkB = gen_bank()
                for kc in range(8):
                    P.op("pe", lambda e, kc=kc, fo=fo, bk=bk, tc=tc: e.matmul(
                        bk[:, :], lhsT=wout[:, kc, fo * 128:(fo + 1) * 128], rhs=mixT[:, kc, tcs(tc)], start=(kc == 0), stop=(kc == 7)),
                        reads=[woB, B_mix[kc]], writes=[bkB])
                P.op("dve", lambda e, fo=fo, bk=bk, tc=tc: e.scalar_tensor_tensor(
                    out=xT[:, fo, tcs(tc)], in0=bk[:, :], scalar=modT[:, l, 16 + fo, b:b + 1], in1=xT[:, fo, tcs(tc)],
                    op0=ALU.mult, op1=ALU.add), reads=[bkB, B_mod], writes=[B_xT[tc]])

    def even_layer(l, b):
        e_ = l // 2
        w_in = ewin_d[e_]
        phase()
        make_hT(l, b)
        phase()
        wsl = [A.alloc([128, 8, 4, 128], BF16) for _ in range(2)]
        wB = [Buf("wsl0"), Buf("wsl1")]
        qT2 = A.alloc([128, 2, S], BF16)
        kT = A.alloc([128, S], BF16)
        gaT = A.alloc([128, S], BF16)
        v = A.alloc([128, 16, 129], BF16)
        bt32 = A.alloc([128, BTW], F32)
        ebt = A.alloc([128, BTW], BF16)
        Bq, Bk, Bga, Bv, Bbt = Buf("q"), Buf("k"), Buf("ga"), Buf("v"), Buf("bt")
        NE = 3
        Es = [A.alloc([128, 512], BF16) for _ in range(NE)]
        EB = [Buf(f"E{i}") for i in range(NE)]
        P.op("dve", lambda e: e.memset(qT2[64:128, 0, :], 0.0), writes=[Bq])
        P.op("dve", lambda e: e.memset(qT2[0:64, 1, :], 0.0), writes=[Bq])
        o0s = [[A.alloc([128, 128], F32) for _ in range(2)] for _ in range(2)]
        ds = [[A.alloc([128, 128], F32) for _ in range(2)] for _ in range(2)]
        dns = [[A.alloc([128, 128], BF16) for _ in range(2)] for _ in range(2)]
        smalls = [A.alloc([128, 16], F32) for _ in range(2)]
        Bposts = [[Buf(f"post{i}{j}") for j in range(2)] for i in range(2)]
        Bsms = [Buf("small0"), Buf("small1")]
        neglam = lamc[:, 8 + e_:9 + e_]
        sgc = lamc[:, 10 + e_:11 + e_]
        for h in range(4):
            sl = h % 2
            load_w(wsl[sl], wB[sl], w_in, [h * 128, 512 + h * 128, 1024 + h * 128, 1536 + h * 128], 128)
            P.dma("sp", [(bt32[:, :], t5_d[h])], writes=[Bbt])
            P.op("act", lambda e: e.activation(out=ebt[:, :], in_=bt32[:, :], func=AF.Exp), reads=[Bbt], writes=[Bbt])
            P.op("dve", lambda e: e.memset(v[:, :, 128:129], 1.0), writes=[Bv])
            pend = []

            def ev_q(tc, bk, bkB):
                for m_ in range(2):
                    P.op("dve", lambda e, m_=m_: e.tensor_scalar(out=qT2[m_ * 64:(m_ + 1) * 64, m_, tcs(tc)], in0=bk[m_ * 64:(m_ + 1) * 64, :],
                                                                 scalar1=0.125, scalar2=None, op0=ALU.mult),
                         reads=[bkB], writes=[Bq])

            def ev_k(tc, bk, bkB):
                P.op("dve", lambda e: e.tensor_copy(out=kT[:, tcs(tc)], in_=bk[:, :]), reads=[bkB], writes=[Bk])

            def ev_g(tc, bk, bkB):
                P.op("act", lambda e: e.activation(out=gaT[:, tcs(tc)], in_=bk[:, :], func=AF.Silu), reads=[bkB], writes=[Bga])

            proj_feat(wsl[sl], wB[sl], 0, ev_q)
            proj_feat(wsl[sl], wB[sl], 1, ev_k)
            proj_feat(wsl[sl], wB[sl], 3, ev_g)
            for g4 in range(4):
                bk, bkB = gen_bank()
                for i in range(4):
                    tt = g4 * 4 + i
                    for c in range(8):
                        P.op("pe", lambda e, c=c, bk=bk, i=i, tt=tt, sl=sl: e.matmul(
                            bk[:, i * 128:(i + 1) * 128], lhsT=hT[:, c, tt * 128:(tt + 1) * 128], rhs=wsl[sl][:, c, 2, :],
                            start=(c == 0), stop=(c == 7)), reads=[wB[sl], B_hT[tt // 4]], writes=[bkB])
                P.op("act", lambda e, bk=bk, g4=g4: e.activation(
                    out=v[:, g4 * 4:(g4 + 1) * 4, 0:128], in_=bk[:, :].rearrange("p (a b) -> p a b", a=4), func=AF.Copy),
                    reads=[bkB], writes=[Bv])
            for qc in range(8):
                Ob = [(banks[4], bankB[4], banks[5], bankB[5]), (banks[6], bankB[6], banks[7], bankB[7])][qc % 2]
                O = [Ob[0], Ob[2]]
                OB = [Ob[1], Ob[3]]

                def qk(kc, qc=qc):
                    si = kc % 2
                    ei = kc % NE
                    Sb, SB = banks[2 + si], bankB[2 + si]
                    Dd = kc * 128 - qc * 256
                    for m in range(2):
                        P.op("pe", lambda e, m=m, Sb=Sb, kc=kc: e.matmul(
                            Sb[:, m * 256:(m + 1) * 256], lhsT=kT[:, kc * 128:(kc + 1) * 128],
                            rhs=qT2[:, m, qc * 256:(qc + 1) * 256], start=True, stop=True),
                            reads=[Bq, Bk], writes=[SB])
                    if Dd >= 384 or Dd <= -256:
                        cb_ = bt32[:, 0:1] if Dd >= 384 else bt32[:, BTW - 1:BTW]
                        P.op("act", lambda e, Sb=Sb, ei=ei, cb_=cb_: e.activation(out=Es[ei], in_=Sb[:, :], func=AF.Exp, bias=cb_, scale=1.0),
                             reads=[SB, Bbt], writes=[EB[ei]])
                    else:
                        off = 384 - Dd
                        P.op("act", lambda e, Sb=Sb, ei=ei: e.activation(out=Es[ei], in_=Sb[:, :], func=AF.Exp),
                             reads=[SB], writes=[EB[ei]])
                        P.op("dve", lambda e, ei=ei, off=off: e.tensor_tensor(
                            out=Es[ei].rearrange("p (m q) -> p m q", m=2), in0=Es[ei].rearrange("p (m q) -> p m q", m=2),
                            in1=ebt[:, off:off + 256].unsqueeze(1).to_broadcast([128, 2, 256]), op=ALU.mult),
                            reads=[Bbt], writes=[EB[ei]])

                def av(kc):
                    si = kc % NE
                    for m in range(2):
                        for qs in range(2):
                            P.op("pe", lambda e, m=m, qs=qs, si=si, kc=kc, O=O: e.matmul(
                                O[m][:, qs * 129:(qs + 1) * 129], lhsT=Es[si][:, m * 256 + qs * 128:m * 256 + (qs + 1) * 128],
                                rhs=v[:, kc, :], start=(kc == 0 and qs == 0), stop=(kc == 15), skip_group_check=True),
                                reads=[EB[si], Bv], writes=[OB[m]])

                qk(0)
                for kc in range(16):
                    if kc + 1 < 16:
                        qk(kc + 1)
                    av(kc)
                    if kc in (1, 6, 9, 12) and pend:
                        pend[0].pop(0)()
                        if not pend[0]:
                            pend.pop(0)
                O3 = [O[m][:, 0:258].rearrange("p (a b) -> p a b", a=2) for m in range(2)]
                par = qc % 2
                sm = smalls[par]
                Bs_ = Bsms[par]
                o0q, dq, dnq, Bpq = o0s[par], ds[par], dns[par], Bposts[par]

                def stage_a(O3=O3, OB=OB, sm=sm, Bs_=Bs_, o0q=o0q, dq=dq, Bpq=Bpq):
                    for m in range(2):
                        P.op("dve", lambda e, m=m: e.reciprocal(out=sm[:, m * 2:(m + 1) * 2], in_=O3[m][:, :, 128]),
                             reads=[OB[m]], writes=[Bs_])
                    P.op("dve", lambda e: e.tensor_scalar(out=sm[:, 4:6], in0=sm[:, 2:4], scalar1=neglam, scalar2=None, op0=ALU.mult),
                         reads=[Bs_, B_par], writes=[Bs_])
                    for qs in range(2):
                        P.op("dve", lambda e, qs=qs: e.tensor_scalar(
                            out=o0q[qs], in0=O3[0][:, qs, 0:128], scalar1=sm[:, qs:qs + 1], scalar2=None, op0=ALU.mult),
                            reads=[OB[0], Bs_], writes=[Bpq[qs]])
                        P.op("dve", lambda e, qs=qs: e.scalar_tensor_tensor(
                            out=dq[qs], in0=O3[1][:, qs, 0:128], scalar=sm[:, 4 + qs:5 + qs], in1=o0q[qs], op0=ALU.mult, op1=ALU.add),
                            reads=[OB[1], Bs_, Bpq[qs]], writes=[Bpq[qs]])
                        P.op("dve", lambda e, qs=qs: e.tensor_tensor(out=o0q[qs], in0=dq[qs], in1=dq[qs], op=ALU.mult),
                             reads=[Bpq[qs]], writes=[Bpq[qs]])
                        P.op("dve", lambda e, qs=qs: e.reduce_sum(out=sm[:, 6 + qs:7 + qs], in_=o0q[qs], axis=AX.X),
                             reads=[Bpq[qs]], writes=[Bs_])

                def stage_b(sm=sm, Bs_=Bs_):
                    P.op("act", lambda e: e.activation(out=sm[:, 8:10], in_=sm[:, 6:8], func=AF.Ln, scale=1.0 / 128, bias=eps_col),
                         reads=[Bs_, B_par], writes=[Bs_])
                    P.op("act", lambda e: e.activation(out=sm[:, 10:12], in_=sm[:, 8:10], func=AF.Exp, scale=-0.5),
                         reads=[Bs_], writes=[Bs_])

                def stage_c(sm=sm, Bs_=Bs_, dq=dq, dnq=dnq, Bpq=Bpq):
                    for qs in range(2):
                        P.op("dve", lambda e, qs=qs: e.tensor_scalar(out=dnq[qs], in0=dq[qs], scalar1=sm[:, 10 + qs:11 + qs], scalar2=None, op0=ALU.mult),
                             reads=[Bpq[qs], Bs_], writes=[Bpq[qs]])

                def stage_d(dnq=dnq, Bpq=Bpq, qc=qc, h=h):
                    for qs in range(2):
                        qb = qc * 2 + qs
                        bk, bkB = gen_bank()
                        bkb = bk[:, :].bitcast(BF16)
                        P.op("pe", lambda e, qs=qs, bkb=bkb: e.transpose(out=bkb[:, 0:128], in_=dnq[qs], identity=identb[:]),
                             reads=[Bpq[qs], B_const], writes=[bkB])
                        P.op("dve", lambda e, bkb=bkb, qb=qb: e.scalar_tensor_tensor(
                            out=mixT[:, h, qb * 128:(qb + 1) * 128], in0=bkb[:, 0:128], scalar=sgc, in1=gaT[:, qb * 128:(qb + 1) * 128],
                            op0=ALU.mult, op1=ALU.mult), reads=[bkB, Bga, B_par], writes=[B_mix[h]])
                pend.append([stage_a, stage_b, stage_c, stage_d])
            for st in pend:
                for f_ in st:
                    f_()
            pend.clear()

        phase()
        wsl2 = A.alloc([128, 8, 2, 128], BF16)
        w2B = Buf("wsl2")
        wbd = A.alloc([128, 16, 128], BF16)
        wbdB = Buf("wbd")
        P.dma("pool", [(wbd[:, :, :], wbd_d[e_])], writes=[wbdB])
        nsp16 = A.alloc([128, 8], F32)
        Bn16 = Buf("nsp16")
        P.op("dve", lambda e: e.tensor_scalar(out=nsp16[:, :].rearrange("p (r c) -> p r c", r=2), in0=nsp8[:, e_, :, :],
                                              scalar1=2.0, scalar2=None, op0=ALU.mult), reads=[B_par], writes=[Bn16])
        xb = A.alloc([128, S], F32)
        xc = A.alloc([128, S], F32)
        xcb = A.alloc([128, S], BF16)
        gb = A.alloc([128, S], BF16)
        Aa = A.alloc([128, S], F32)
        M_ = A.alloc([128, S], F32)
        T3 = A.alloc([128, S], F32)
        T4 = A.alloc([128, S], F32)
        Bxb, Bxc, Bgb = Buf("xb"), Buf("xc"), Buf("gb")
        Bxcb = [Buf(f"xcb{i}") for i in range(4)]
        BA = [Buf(f"A{i}") for i in range(4)]
        BM = [Buf(f"M{i}") for i in range(4)]
        BT3 = [Buf(f"T3{i}") for i in range(4)]
        BT4 = [Buf(f"T4{i}") for i in range(4)]

        def rev(ap2d):
            (ps_, pn_), (fs_, fn_) = ap2d.ap
            return bass.AP(ap2d.tensor, ap2d.offset + (fn_ - 1) * fs_, [[ps_, pn_], [-fs_, fn_]])

        for c in range(4):
            load_w(wsl2, w2B, w_in, [2048 + c * 128, 2560 + c * 128], 128)

            def ev_xb(tc, bk, bkB):
                P.op("dve", lambda e: e.tensor_copy(out=xb[:, tcs(tc)], in_=bk[:, :]), reads=[bkB], writes=[Bxb])

            def ev_gb(tc, bk, bkB):
                P.op("act", lambda e: e.activation(out=gb[:, tcs(tc)], in_=bk[:, :], func=AF.Silu), reads=[bkB], writes=[Bgb])

            proj_feat(wsl2, w2B, 0, ev_xb)
            proj_feat(wsl2, w2B, 1, ev_gb)
            w0, w1, w2, w3 = (cw[:, e_, c, j:j + 1] for j in range(4))
            cbc = cb[:, e_, c:c + 1]
            P.op("dve", lambda e, w2=w2, cbc=cbc: e.tensor_scalar(out=xc[:, :], in0=xb[:, :], scalar1=w2, scalar2=cbc,
                                                                  op0=ALU.mult, op1=ALU.add), reads=[Bxb, B_par], writes=[Bxc])
            P.op("dve", lambda e, w0=w0: e.scalar_tensor_tensor(out=xc[:, 2:S], in0=xb[:, 0:S - 2], scalar=w0, in1=xc[:, 2:S],
                                                                op0=ALU.mult, op1=ALU.add), reads=[Bxb, B_par, Bxc], writes=[Bxc])
            P.op("dve", lambda e, w1=w1: e.scalar_tensor_tensor(out=xc[:, 1:S], in0=xb[:, 0:S - 1], scalar=w1, in1=xc[:, 1:S],
                                                                op0=ALU.mult, op1=ALU.add), reads=[Bxb, B_par, Bxc], writes=[Bxc])
            P.op("dve", lambda e, w3=w3: e.scalar_tensor_tensor(out=xc[:, 0:S - 1], in0=xb[:, 1:S], scalar=w3, in1=xc[:, 0:S - 1],
                                                                op0=ALU.mult, op1=ALU.add), reads=[Bxb, B_par, Bxc], writes=[Bxc])
            for tc in range(4):
                P.op("act", lambda e, tc=tc: e.activation(out=xcb[:, tcs(tc)], in_=xc[:, tcs(tc)], func=AF.Copy), reads=[Bxc], writes=[Bxcb[tc]])
            for r in range(2):
                X_, BX = (T3, BT3) if r == 0 else (T4, BT4)
                order = [0, 1, 2, 3] if r == 0 else [3, 2, 1, 0]
                for tc in order:
                    for g_, dst, dB, brow in ((0, Aa, BA, 0), (1, X_, BX, 1)):
                        bk, bkB = gen_bank()
                        P.op("pe", lambda e, bk=bk, g_=g_, r=r, c=c, tc=tc: e.matmul(
                            bk[:, :], lhsT=wbd[:, (g_ * 2 + r) * 4 + c, :], rhs=xcb[:, tcs(tc)], start=True, stop=True),
                            reads=[wbdB, Bxcb[tc]], writes=[bkB])
                        P.op("act", lambda e, bk=bk, dst=dst, brow=brow, r=r, c=c, tc=tc: e.activation(
                            out=dst[:, tcs(tc)], in_=bk[:, :], func=AF.Sigmoid, bias=lb[:, e_, brow, r, c:c + 1], scale=1.0),
                            reads=[bkB, B_par], writes=[dB[tc]])
                for tc in order:
                    P.op("act", lambda e, r=r, c=c, tc=tc: e.activation(out=M_[:, tcs(tc)], in_=Aa[:, tcs(tc)], func=AF.Exp,
                                                                        scale=nsp16[:, r * 4 + c:r * 4 + c + 1]),
                         reads=[BA[tc], Bn16], writes=[BM[tc]])
                    P.op("act", lambda e, r=r, c=c, tc=tc: e.activation(out=Aa[:, tcs(tc)], in_=Aa[:, tcs(tc)], func=AF.Exp,
                                                                        scale=nsp8[:, e_, r, c:c + 1]),
                         reads=[BA[tc], B_par], writes=[BA[tc]])
                for tc in order:
                    P.op("act", lambda e, tc=tc: e.activation(out=M_[:, tcs(tc)], in_=M_[:, tcs(tc)], func=AF.Sqrt, scale=-1.0, bias=1.0),
                         reads=[BM[tc]], writes=[BM[tc]])
                for k_, tc in enumerate(order):
                    P.op("dve", lambda e, X_=X_, tc=tc: e.tensor_tensor(out=M_[:, tcs(tc)], in0=M_[:, tcs(tc)], in1=X_[:, tcs(tc)], op=ALU.mult),
                         reads=[BM[tc], BX[tc]], writes=[BM[tc]])
                    P.op("dve", lambda e, tc=tc: e.tensor_tensor(out=M_[:, tcs(tc)], in0=M_[:, tcs(tc)], in1=xc[:, tcs(tc)], op=ALU.mult),
                         reads=[BM[tc], Bxc], writes=[BM[tc]])
                    if r == 0:
                        init = 0.0 if k_ == 0 else X_[:, tc * 512 - 1:tc * 512]
                        rd = [BA[tc], BM[tc]] + ([BX[tc - 1]] if k_ > 0 else [])
                        P.op("dve", lambda e, X_=X_, tc=tc, init=init: e.tensor_tensor_scan(
                            out=X_[:, tcs(tc)], data0=Aa[:, tcs(tc)], data1=M_[:, tcs(tc)], initial=init, op0=ALU.mult, op1=ALU.add),
                            reads=rd, writes=[BX[tc]])
                    else:
                        init = 0.0 if k_ == 0 else X_[:, (tc + 1) * 512:(tc + 1) * 512 + 1]
                        rd = [BA[tc], BM[tc]] + ([BX[tc + 1]] if k_ > 0 else [])
                        P.op("dve", lambda e, X_=X_, tc=tc, init=init: e.tensor_tensor_scan(
                            out=rev(X_[:, tcs(tc)]), data0=rev(Aa[:, tcs(tc)]), data1=rev(M_[:, tcs(tc)]), initial=init,
                            op0=ALU.mult, op1=ALU.add), reads=rd, writes=[BX[tc]])
            for tc in range(4):
                P.op("dve", lambda e, tc=tc: e.tensor_tensor(out=T3[:, tcs(tc)], in0=T3[:, tcs(tc)], in1=T4[:, tcs(tc)], op=ALU.add),
                     reads=[BT3[tc], BT4[tc]], writes=[BT3[tc]])
                P.op("dve", lambda e, c=c, tc=tc: e.tensor_tensor(out=mixT[:, 4 + c, tcs(tc)], in0=T3[:, tcs(tc)], in1=gb[:, tcs(tc)], op=ALU.mult),
                     reads=[BT3[tc], Bgb], writes=[B_mix[4 + c]])
        out_proj(l, b, ewout_d[e_])

    def odd_layer(l, b):
        o_ = l // 2
        w_in = owin_d[o_]
        phase()
        make_hT(l, b)
        phase()
        wsl = [A.alloc([128, 8, 4, 128], BF16) for _ in range(2)]
        wB = [Buf("wsl0"), Buf("wsl1")]
        qT2 = A.alloc([128, 2, S], BF16)
        kT = A.alloc([128, S], BF16)
        gT = A.alloc([128, S], BF16)
        v = A.alloc([128, 16, 2, 65], BF16)
        nb32 = A.alloc([128, NT, 2, 128], F32)
        enb = A.alloc([128, NT, 2, 128], BF16)
        Bq, Bk, Bg, Bv, Bnb = Buf("q"), Buf("k"), Buf("g"), Buf("v"), Buf("nb")
        NE = 3
        Es = [A.alloc([128, 512], BF16) for _ in range(NE)]
        EB = [Buf(f"E{i}") for i in range(NE)]
        P.op("dve", lambda e: e.memset(qT2[64:128, 0, :], 0.0), writes=[Bq])
        P.op("dve", lambda e: e.memset(qT2[0:64, 1, :], 0.0), writes=[Bq])
        obf = [A.alloc([128, 128], BF16) for _ in range(2)]
        Bo = [Buf("o0"), Buf("o1")]
        small = A.alloc([128, 4], F32)
        Bsm = Buf("small")
        for hp in range(8):
            sl = hp % 2
            load_w(wsl[sl], wB[sl], w_in, [hp * 128, 1024 + hp * 128, 2048 + hp * 128, 3072 + hp * 128], 128)
            P.dma("sp", [(nb32[:, :, :, :], nab_d[o_, hp])], writes=[Bnb])
            P.op("act", lambda e: e.activation(out=enb[:, :, :, :], in_=nb32[:, :, :, :], func=AF.Exp), reads=[Bnb], writes=[Bnb])
            P.op("dve", lambda e: e.memset(v[:, :, :, 64:65], 1.0), writes=[Bv])
            pend = []

            def ev_q(tc, bk, bkB):
                for m_ in range(2):
                    P.op("dve", lambda e, m_=m_: e.tensor_scalar(out=qT2[m_ * 64:(m_ + 1) * 64, m_, tcs(tc)], in0=bk[m_ * 64:(m_ + 1) * 64, :],
                                                                 scalar1=0.125, scalar2=None, op0=ALU.mult),
                         reads=[bkB], writes=[Bq])

            def ev_k(tc, bk, bkB):
                P.op("dve", lambda e: e.tensor_copy(out=kT[:, tcs(tc)], in_=bk[:, :]), reads=[bkB], writes=[Bk])

            def ev_g(tc, bk, bkB):
                P.op("act", lambda e: e.activation(out=gT[:, tcs(tc)], in_=bk[:, :], func=AF.Silu), reads=[bkB], writes=[Bg])

            proj_feat(wsl[sl], wB[sl], 0, ev_q)
            proj_feat(wsl[sl], wB[sl], 1, ev_k)
            proj_feat(wsl[sl], wB[sl], 3, ev_g)
            for g4 in range(4):
                bk, bkB = gen_bank()
                for i in range(4):
                    tt = g4 * 4 + i
                    for c in range(8):
                        P.op("pe", lambda e, c=c, bk=bk, i=i, tt=tt, sl=sl: e.matmul(
                            bk[:, i * 128:(i + 1) * 128], lhsT=hT[:, c, tt * 128:(tt + 1) * 128], rhs=wsl[sl][:, c, 2, :],
                            start=(c == 0), stop=(c == 7)), reads=[wB[sl], B_hT[tt // 4]], writes=[bkB])
                P.op("act", lambda e, bk=bk, g4=g4: e.activation(
                    out=v[:, g4 * 4:(g4 + 1) * 4, :, 0:64], in_=bk[:, :].rearrange("p (a b c) -> p a b c", a=4, b=2), func=AF.Copy),
                    reads=[bkB], writes=[Bv])
            work = []
            for rp in range(16):
                ch = NA_PLAN[rp]
                groups = [ch[i:i + 2] for i in range(0, len(ch), 2)]
                for gi, grp in enumerate(groups):
                    work.append((rp, gi, len(groups), grp))
            state = {"n": 0}

            def qk(w):
                rp, gi, ng_, grp = w
                si = state["n"] % 2
                ei = state["n"] % NE
                state["n"] += 1
                Sb, SB = banks[2 + si], bankB[2 + si]
                for ci, (j, t) in enumerate(grp):
                    P.op("pe", lambda e, Sb=Sb, ci=ci, j=j, rp=rp: e.matmul(
                        Sb[:, ci * 256:(ci + 1) * 256].rearrange("p (h q) -> p h q", h=2), lhsT=kT[:, j * 128:(j + 1) * 128],
                        rhs=qT2[:, :, rp * 128:(rp + 1) * 128], start=True, stop=True),
                        reads=[Bq, Bk], writes=[SB])
                ncol = len(grp) * 256
                P.op("act", lambda e, Sb=Sb, ei=ei, ncol=ncol: e.activation(out=Es[ei][:, 0:ncol], in_=Sb[:, 0:ncol], func=AF.Exp),
                     reads=[SB], writes=[EB[ei]])
                if len(grp) == 2 and grp[1][1] == grp[0][1] + 1:
                    t0_ = grp[0][1]
                    P.op("dve", lambda e, ei=ei, t0_=t0_: e.tensor_tensor(
                        out=Es[ei][:, 0:512].rearrange("p (c h q) -> p c h q", c=2, h=2),
                        in0=Es[ei][:, 0:512].rearrange("p (c h q) -> p c h q", c=2, h=2),
                        in1=enb[:, t0_:t0_ + 2, :, :], op=ALU.mult), reads=[Bnb], writes=[EB[ei]])
                else:
                    for ci, (j, t) in enumerate(grp):
                        P.op("dve", lambda e, ei=ei, ci=ci, t=t: e.tensor_tensor(
                            out=Es[ei][:, ci * 256:(ci + 1) * 256].rearrange("p (h q) -> p h q", h=2),
                            in0=Es[ei][:, ci * 256:(ci + 1) * 256].rearrange("p (h q) -> p h q", h=2),
                            in1=enb[:, t, :, :], op=ALU.mult), reads=[Bnb], writes=[EB[ei]])
                return ei

            def av(w, si):
                rp, gi, ng_, grp = w
                Ob, OBf = banks[4 + rp % 4], bankB[4 + rp % 4]
                for ci, (j, t) in enumerate(grp):
                    for hh in range(2):
                        col = (ci * 2 + hh) * 128
                        first = (gi == 0 and ci == 0)
                        last = (gi == ng_ - 1 and ci == len(grp) - 1)
                        P.op("pe", lambda e, Ob=Ob, col=col, hh=hh, j=j, si=si, first=first, last=last: e.matmul(
                            Ob[:, hh * 65:(hh + 1) * 65], lhsT=Es[si][:, col:col + 128], rhs=v[:, j, hh, :],
                            start=(first and hh == 0), stop=last, skip_group_check=True),
                            reads=[EB[si], Bv], writes=[OBf])
                for f_ in pend:
                    f_()
                pend.clear()
                if gi == ng_ - 1:
                    post(rp, Ob, OBf)

            def post(rp, Ob, OBf, hp=hp):
                pi = rp % 2
                O3 = Ob[:, 0:130].rearrange("p (a b) -> p a b", a=2)
                P.op("dve", lambda e: e.reciprocal(out=small[:, 0:2], in_=O3[:, :, 64]), reads=[OBf], writes=[Bsm])
                P.op("dve", lambda e: e.tensor_tensor(out=obf[pi].rearrange("p (h d) -> p h d", h=2), in0=O3[:, :, 0:64],
                                                      in1=small[:, 0:2].unsqueeze(2).to_broadcast([128, 2, 64]), op=ALU.mult),
                     reads=[OBf, Bsm], writes=[Bo[pi]])

                def post_b():
                    bk, bkB = gen_bank()
                    bkb = bk[:, :].bitcast(BF16)
                    P.op("pe", lambda e: e.transpose(out=bkb[:, 0:128], in_=obf[pi], identity=identb[:]), reads=[Bo[pi], B_const], writes=[bkB])
                    P.op("dve", lambda e: e.tensor_tensor(out=mixT[:, hp, rp * 128:(rp + 1) * 128], in0=bkb[:, 0:128],
                                                          in1=gT[:, rp * 128:(rp + 1) * 128], op=ALU.mult),
                         reads=[bkB, Bg], writes=[B_mix[hp]])
                pend.append(post_b)

            si_cur = qk(work[0])
            for wi in range(len(work)):
                si_next = None
                if wi + 1 < len(work):
                    si_next = qk(work[wi + 1])
                av(work[wi], si_cur)
                si_cur = si_next
            for f_ in pend:
                f_()
            pend.clear()
        out_proj(l, b, owout_d[o_])

    final_toks = []
    for b in range(NB):
        phase()
        xtok = [A.alloc([128, D], F32) for _ in range(3)]
        xB = [Buf(f"xtok{i}") for i in range(3)]
        for tt in range(16):
            sl = tt % 3
            P.dma("sp", [(xtok[sl][:, :], x_d[b, tt * 128:(tt + 1) * 128, :])], writes=[xB[sl]])
            for half in range(2):
                bk, bkB = gen_bank()
                for i in range(4):
                    c = half * 4 + i
                    P.op("pe", lambda e, bk=bk, i=i, c=c, sl=sl: e.transpose(out=bk[:, i * 128:(i + 1) * 128],
                                                                             in_=xtok[sl][:, c * 128:(c + 1) * 128], identity=ident32[:]),
                         reads=[xB[sl], B_const], writes=[bkB])
                eng = "dve" if half == 0 else "act"
                if eng == "dve":
                    P.op("dve", lambda e, bk=bk, half=half, tt=tt: e.tensor_copy(
                        out=xT[:, half * 4:(half + 1) * 4, tt * 128:(tt + 1) * 128], in_=bk[:, :].rearrange("p (a b) -> p a b", a=4)),
                        reads=[bkB], writes=[B_xT[tt // 4]])
                else:
                    P.op("act", lambda e, bk=bk, half=half, tt=tt: e.activation(
                        out=xT[:, half * 4:(half + 1) * 4, tt * 128:(tt + 1) * 128], in_=bk[:, :].rearrange("p (a b) -> p a b", a=4), func=AF.Copy),
                        reads=[bkB], writes=[B_xT[tt // 4]])
        for l in range(nlayers):
            if l % 2 == 0:
                even_layer(l, b)
            else:
                odd_layer(l, b)
        phase()
        sq_slots = [A.alloc([128, 512], BF16) for _ in range(4)]
        sqB = [Buf(f"sq{i}") for i in range(4)]
        lnv = A.alloc([128, 512], F32)
        rstd = A.alloc([128, 512], F32)
        rB = Buf("rstd")
        yT = A.alloc([128, 8, 512], F32)
        yB = Buf("yT")
        otok = [A.alloc([128, D], F32) for _ in range(2)]
        oB = [Buf("otok0"), Buf("otok1")]
        k = 0
        for tc in range(4):
            rms_stats(tc, (sq_slots, sqB, lnv, rstd, rB), [B_xT[tc]])
            for c in range(8):
                P.op("dve", lambda e, c=c, tc=tc: e.scalar_tensor_tensor(
                    out=yT[:, c, :], in0=xT[:, c, tcs(tc)], scalar=fg[:, c:c + 1], in1=rstd, op0=ALU.mult, op1=ALU.mult),
                    reads=[B_xT[tc], rB, B_par], writes=[yB])
            for i4 in range(4):
                tt = tc * 4 + i4
                sl = k % 2
                k += 1
                for half in range(2):
                    bk, bkB = gen_bank()
                    for i in range(4):
                        c = half * 4 + i
                        P.op("pe", lambda e, bk=bk, i=i, c=c, i4=i4: e.transpose(
                            out=bk[:, i * 128:(i + 1) * 128], in_=yT[:, c, i4 * 128:(i4 + 1) * 128], identity=ident32[:]),
                            reads=[yB, B_const], writes=[bkB])
                    if half == 0:
                        P.op("dve", lambda e, bk=bk, sl=sl, half=half: e.tensor_copy(out=otok[sl][:, half * 512:(half + 1) * 512], in_=bk[:, :]),
                             reads=[bkB], writes=[oB[sl]])
                    else:
                        P.op("act", lambda e, bk=bk, sl=sl, half=half: e.activation(out=otok[sl][:, half * 512:(half + 1) * 512], in_=bk[:, :], func=AF.Copy),
                             reads=[bkB], writes=[oB[sl]])
                final_toks.append(P.dma("sp", [(out_d[b, tt * 128:(tt + 1) * 128, :], otok[sl][:, :])], reads=[oB[sl]]))
    P.emit(final_toks)
    return nc


_NC_CACHE = {}


def _prep_shared(inp):
    f = lambda a: np.ascontiguousarray(np.asarray(a, dtype=np.float32))
    sh = {}
    sh["ada_w"] = f(inp["ada_w"])
    sh["ada_b_l"] = f(np.transpose(np.asarray(inp["ada_b"]).reshape(4, 24, 128), (2, 0, 1)))
    sh["norm_g_l"] = f(np.transpose(np.asarray(inp["norm_g"]).reshape(4, 8, 128), (2, 0, 1)))
    sh["final_g_l"] = f(np.asarray(inp["final_g"]).reshape(8, 128).T)
    sh["even_w_in"] = f(inp["even_w_in"])
    sh["even_w_out"] = f(inp["even_w_out"])
    sh["odd_w_in"] = f(inp["odd_w_in"])
    sh["odd_w_out"] = f(inp["odd_w_out"])
    sh["t5bt"] = _t5_bias_host(np.asarray(inp["t5_table"], dtype=np.float32))
    sh["da_lam_f"] = f(np.asarray(inp["da_lam"]).reshape(1, 512))
    sh["subln_g_l"] = f(np.asarray(inp["da_subln_g"]).T)
    sh["conv_w_l"] = f(np.transpose(np.asarray(inp["lru_conv_w"]).reshape(2, 4, 4, 128), (3, 0, 2, 1)))
    sh["conv_b_l"] = f(np.transpose(np.asarray(inp["lru_conv_b"]).reshape(2, 4, 128), (2, 0, 1)))
    wa = np.asarray(inp["lru_w_a"], dtype=np.float32)
    wx = np.asarray(inp["lru_w_x"], dtype=np.float32)
    wbd = np.zeros((2, 128, 2, 2, 4, 128), dtype=np.float32)
    for g_, w in enumerate((wa, wx)):
        for c in range(4):
            for half in range(2):
                blk = w[:, :, 2 * c + half]
                wbd[:, half * 64:(half + 1) * 64, g_, :, c, half * 64:(half + 1) * 64] = np.transpose(blk, (0, 2, 1, 3))
    sh["lru_wbd"] = np.ascontiguousarray(wbd.reshape(2, 128, 16, 128))
    lbias = np.stack([np.asarray(inp["lru_b_a"]), np.asarray(inp["lru_b_x"]), np.asarray(inp["lru_lambda"])], axis=1)
    sh["lru_bias_l"] = f(np.transpose(lbias.reshape(2, 3, 2, 4, 128), (4, 0, 1, 2, 3)))
    sh["na_bias"] = _na_bias_host(np.asarray(inp["na_rpb"], dtype=np.float32))
    return sh


def kernel(nlayers=4, cores=8, **inp):
    x = np.asarray(inp["x"], dtype=np.float32)
    c = np.asarray(inp["c"], dtype=np.float32)
    sh = _prep_shared(inp)
    key = nlayers
    if key not in _NC_CACHE:
        _NC_CACHE[key] = build_program(nlayers)
    nc = _NC_CACHE[key]
    in_maps = []
    for i in range(cores):
        m = dict(sh)
        m["x"] = np.ascontiguousarray(x[NB * i:NB * (i + 1)])
        cc = c[NB * i:NB * (i + 1)]
        m["c_l"] = np.ascontiguousarray(np.transpose(cc.T.reshape(8, 128, NB), (1, 0, 2)))
        in_maps.append(m)
    res = run_bass_kernel_spmd(nc, in_maps, core_ids=list(range(cores)))
    out = np.concatenate([np.asarray(r["out"], dtype=np.float32) for r in res.results], axis=0)
    return out
```
